# Optimizing a Trainium2 kernel written in Bass

```python
import math, functools
import jax, jax.numpy as jnp
from jax import lax
import numpy as np

D_MODEL = 1024
BATCH = 4
SEQ = 4096
DEPTH = 1
DEC_BATCH = 32
DEC_SEQ = 32
PAST_LEN = 4096

CHUNK = 64
WINDOW = 128
D_MIX = D_MODEL
ATTN_WIDTH = D_MIX // 2
GLA_WIDTH = D_MIX - ATTN_WIDTH
HEAD_DIM = 64
N_HEADS = ATTN_WIDTH // HEAD_DIM
N_KV_HEADS = 2
GQA_GROUP = N_HEADS // N_KV_HEADS
ROT_DIM = HEAD_DIM // 4
ROPE_THETA = 500000.0
GLA_HEADS = 4
GLA_DV = GLA_WIDTH // GLA_HEADS
GLA_DK = GLA_DV // 2
GLA_RANK = 16
GATE_TAU = 16.0
NORM_EPS = 1e-6
IN_SIZES = (N_HEADS * HEAD_DIM, N_KV_HEADS * HEAD_DIM, N_KV_HEADS * HEAD_DIM, ATTN_WIDTH,
            GLA_HEADS * GLA_DK, GLA_HEADS * GLA_DK, GLA_WIDTH, GLA_WIDTH, GLA_RANK)
D_IN = sum(IN_SIZES)

kernel_name = 'hymba_swa_sink_gla_streaming_step'


def _rms(x, w):
    x32 = x.astype(jnp.float32)
    y = x32 * lax.rsqrt(jnp.mean(x32 * x32, axis=-1, keepdims=True) + NORM_EPS)
    return (y * w.astype(jnp.float32)).astype(x.dtype)


def _rotary(x, pos):
    half = ROT_DIM // 2
    inv = ROPE_THETA ** (-jnp.arange(half, dtype=jnp.float32) * (2.0 / ROT_DIM))
    ang = pos.astype(jnp.float32)[:, None] * inv[None, :]
    cos = jnp.cos(ang)[:, None, :]
    sin = jnp.sin(ang)[:, None, :]
    xf = x.astype(jnp.float32)
    x1 = xf[..., :half]
    x2 = xf[..., half:ROT_DIM]
    out = jnp.concatenate([x1 * cos - x2 * sin, x1 * sin + x2 * cos, xf[..., ROT_DIM:]], axis=-1)
    return out.astype(x.dtype)


def _sink_attend(q, k, v, valid, sinks):
    B, N, Cq = q.shape[:3]
    s = jnp.einsum('bnqkgd,bnskd->bnkgqs', q, k).astype(jnp.float32) * (HEAD_DIM ** -0.5)
    s = jnp.where(valid[None, :, None, None, None, :], s, -1e30)
    sink = sinks.astype(jnp.float32).reshape(1, 1, N_KV_HEADS, GQA_GROUP, 1, 1)
    m = jnp.maximum(jnp.max(s, axis=-1, keepdims=True), sink)
    p = jnp.exp(s - m)
    probs = p / (jnp.sum(p, axis=-1, keepdims=True) + jnp.exp(sink - m))
    o = jnp.einsum('bnkgqs,bnskd->bnqkgd', probs.astype(v.dtype), v)
    return o.reshape(B, N * Cq, N_HEADS * HEAD_DIM)


def _attend_prompt(q, k, v, sinks):
    B, T = q.shape[:2]
    N = T // CHUNK
    nb = WINDOW // CHUNK
    qc = q.reshape(B, N, CHUNK, N_KV_HEADS, GQA_GROUP, HEAD_DIM)

    def band(t):
        tc = t.reshape(B, N, CHUNK, N_KV_HEADS, HEAD_DIM)
        tp = jnp.pad(tc, ((0, 0), (nb, 0), (0, 0), (0, 0), (0, 0)))
        return jnp.concatenate([tp[:, j:j + N] for j in range(nb + 1)], axis=2)

    key_chunk = jnp.arange(N)[:, None] - nb + jnp.arange((nb + 1) * CHUNK)[None, :] // CHUNK
    return _sink_attend(qc, band(k), band(v), key_chunk >= 0, sinks)


def _attend_sample(q, k, v, sinks, cache_k, cache_v):
    B, T = q.shape[:2]
    kk = jnp.concatenate([cache_k.astype(k.dtype), k], axis=1)[:, None]
    vv = jnp.concatenate([cache_v.astype(v.dtype), v], axis=1)[:, None]
    valid = jnp.ones((1, kk.shape[2]), dtype=bool)
    return _sink_attend(q.reshape(B, 1, T, N_KV_HEADS, GQA_GROUP, HEAD_DIM), kk, vv, valid, sinks)


def _gla_chunked(q, k, v, log_a, s0, chunk):
    B, T, H, DK = q.shape
    DV = v.shape[-1]
    N = T // chunk
    f = lambda t: t.astype(jnp.float32).reshape(B, N, chunk, H, t.shape[-1])
    q, k, v, g = f(q) * (GLA_DK ** -0.5), f(k), f(v), f(log_a)
    b = jnp.cumsum(g, axis=2)
    b_last = b[:, :, -1:]
    q_in = q * jnp.exp(b)
    A = jnp.einsum('bnihd,bnjhd->bnhij', q_in, k * jnp.exp(-b))
    causal = jnp.tril(jnp.ones((chunk, chunk), dtype=bool))
    A = jnp.where(causal, A, 0.0)
    o_intra = jnp.einsum('bnhij,bnjhe->bnihe', A, v)
    u = jnp.einsum('bnjhd,bnjhe->bnhde', k * jnp.exp(b_last - b), v)
    decay = jnp.exp(b_last[:, :, 0])

    def step(s, inp):
        dec, uu = inp
        return dec[..., None] * s + uu, s

    s_final, s_prev = lax.scan(step, s0.astype(jnp.float32), (jnp.moveaxis(decay, 1, 0), jnp.moveaxis(u, 1, 0)))
    o_inter = jnp.einsum('bnihd,bnhde->bnihe', q_in, jnp.moveaxis(s_prev, 0, 1))
    return (o_intra + o_inter).reshape(B, T, H, DV), s_final


def _layer(x, pos, attend, s0, gla_chunk, norm_pre_w, w_in, attn_sinks, w_gk_up, b_gk, gla_norm_w, w_out, norm_post_w):
    B, T, _ = x.shape
    h = _rms(x, norm_pre_w)
    z = h @ w_in
    idx = np.cumsum(IN_SIZES)[:-1].tolist()
    aq, ak, av, ag, gq, gk, gv, gg, glr = jnp.split(z, idx, axis=-1)
    aq = _rotary(aq.reshape(B, T, N_HEADS, HEAD_DIM), pos)
    ak = _rotary(ak.reshape(B, T, N_KV_HEADS, HEAD_DIM), pos)
    av = av.reshape(B, T, N_KV_HEADS, HEAD_DIM)
    attn = attend(aq, ak, av, attn_sinks) * jax.nn.silu(ag)
    log_a = jax.nn.log_sigmoid((glr @ w_gk_up + b_gk).astype(jnp.float32)) / GATE_TAU
    o, s_new = _gla_chunked(gq.reshape(B, T, GLA_HEADS, GLA_DK), gk.reshape(B, T, GLA_HEADS, GLA_DK),
                            gv.reshape(B, T, GLA_HEADS, GLA_DV), log_a.reshape(B, T, GLA_HEADS, GLA_DK), s0, gla_chunk)
    o = _rms(o.astype(x.dtype), gla_norm_w).reshape(B, T, GLA_WIDTH) * jax.nn.silu(gg)
    mix = jnp.concatenate([attn, o], axis=-1) @ w_out
    return x + _rms(mix, norm_post_w), ak, av, s_new


def setup_inputs(seed: int = 0) -> dict:
    key = jax.random.key(seed)
    ks = jax.random.split(key, 13)
    nrm = jax.random.normal
    f32 = jnp.float32
    return {
        'x_prompt': nrm(ks[0], (BATCH, SEQ, D_MODEL), f32),
        'x_sample': nrm(ks[1], (DEC_BATCH, DEC_SEQ, D_MODEL), f32),
        'cache_k': nrm(ks[2], (DEPTH, DEC_BATCH, WINDOW, N_KV_HEADS, HEAD_DIM), f32),
        'cache_v': nrm(ks[3], (DEPTH, DEC_BATCH, WINDOW, N_KV_HEADS, HEAD_DIM), f32),
        'state_gla': 0.5 * nrm(ks[4], (DEPTH, DEC_BATCH, GLA_HEADS, GLA_DK, GLA_DV), f32),
        'norm_pre_w': 1.0 + 0.01 * nrm(ks[5], (DEPTH, D_MODEL), f32),
        'w_in': nrm(ks[6], (DEPTH, D_MODEL, D_IN), f32) * (D_MODEL ** -0.5),
        'attn_sinks': 0.5 * nrm(ks[7], (DEPTH, N_HEADS), f32),
        'w_gk_up': nrm(ks[8], (DEPTH, GLA_RANK, GLA_HEADS * GLA_DK), f32) * (GLA_RANK ** -0.5),
        'b_gk': 0.1 * nrm(ks[9], (DEPTH, GLA_HEADS * GLA_DK), f32),
        'gla_norm_w': 1.0 + 0.01 * nrm(ks[10], (DEPTH, GLA_DV), f32),
        'w_out': nrm(ks[11], (DEPTH, D_MIX, D_MODEL), f32) * (D_MIX ** -0.5),
        'norm_post_w': 1.0 + 0.01 * nrm(ks[12], (DEPTH, D_MODEL), f32),
    }


def reference(x_prompt, x_sample, cache_k, cache_v, state_gla, norm_pre_w, w_in, attn_sinks, w_gk_up, b_gk, gla_norm_w, w_out, norm_post_w):
    B, T_p = x_prompt.shape[:2]
    T_s = x_sample.shape[1]
    pos_p = jnp.arange(T_p)
    pos_s = PAST_LEN + jnp.arange(T_s)
    y_p, y_s = x_prompt, x_sample
    kp_l, vp_l, sp_l, ks_l, vs_l, ss_l = [], [], [], [], [], []
    for l in range(DEPTH):
        w = (norm_pre_w[l], w_in[l], attn_sinks[l], w_gk_up[l], b_gk[l], gla_norm_w[l], w_out[l], norm_post_w[l])
        s0 = jnp.zeros((B, GLA_HEADS, GLA_DK, GLA_DV), jnp.float32)
        y_p, kp, vp, sp = _layer(y_p, pos_p, _attend_prompt, s0, CHUNK, *w)
        att_s = functools.partial(_attend_sample, cache_k=cache_k[l], cache_v=cache_v[l])
        y_s, k_s, v_s, s_s = _layer(y_s, pos_s, att_s, state_gla[l], T_s, *w)
        kp_l.append(kp[:, -WINDOW:])
        vp_l.append(vp[:, -WINDOW:])
        sp_l.append(sp.astype(x_prompt.dtype))
        ks_l.append(k_s)
        vs_l.append(v_s)
        ss_l.append(s_s.astype(state_gla.dtype))
    new_k_prompt = jnp.stack(kp_l)
    new_v_prompt = jnp.stack(vp_l)
    new_state_prompt = jnp.stack(sp_l)
    new_k_sample = jnp.stack(ks_l)
    new_v_sample = jnp.stack(vs_l)
    new_state_sample = jnp.stack(ss_l)
    return (y_p, y_s, new_k_prompt, new_v_prompt, new_state_prompt, new_k_sample, new_v_sample, new_state_sample)
```

```python
import math
from contextlib import ExitStack

import numpy as np
import ml_dtypes
import concourse.bass as bass
import concourse.mybir as mybir
from concourse.bass_utils import run_bass_kernel_spmd

F32 = mybir.dt.float32
BF = mybir.dt.bfloat16
AF = mybir.ActivationFunctionType
ALU = mybir.AluOpType

NCORES = 8
D = 1024
DIN = 2832
NPRE = 8
NMAIN = 16
EPS = 1e-6
CQ, CK, CV, CAG, CGQ, CGK, CGV, CGG, CLR = 0, 512, 640, 768, 1280, 1536, 1792, 2304, 2816
WA0, WA1 = 1280, 2832
WB0, WB1 = 0, 1280
LN8 = math.log(0.125)
SKIP_SAMPLE = False
PIPELINE = True
SCHED = 2
STOP_AT = 0
MAXOPS = None
SKIPOPS = ()
JUNKMODE = 0


def th(t):
    return t.tensor if hasattr(t, "tensor") else t


class _Rec:
    def __init__(self):
        self.calls = []

    def __getattr__(self, name):
        if name.startswith("__"):
            raise AttributeError(name)

        def f(*a, **k):
            self.calls.append((name, a, k))
            return self
        return f


def _fsz(ap, skip=1):
    n = 1
    for _, c in list(ap.ap)[skip:]:
        n *= c
    return n


def est_cost(eng, fn):
    if fn is None:
        return 0.0
    rec = _Rec()
    try:
        fn(rec)
        name, a, k = rec.calls[0]
        if name == "matmul":
            rhs = k.get("rhs", a[2] if len(a) > 2 else None)
            n = _fsz(rhs)
            mult = 3.0 if rhs.tensor.dtype == F32 else 1.0
            return max(n, 64) * mult / 2.05 + 15.0
        if name == "transpose":
            return 64.0
        out = k.get("out", a[0] if a else None)
        if name == "dma_start":
            return 2000.0 + _fsz(out, 0) * 4 / 150.0
        n = _fsz(out)
        if eng == "act":
            return 230.0 + 0.85 * n
        if eng == "dve":
            if name in ("tensor_copy", "tensor_scalar"):
                return 90.0 + 0.6 * n
            if name == "reciprocal":
                return 90.0 + 6.5 * n
            return 90.0 + 1.05 * n
        if eng == "pool":
            return 300.0 + 2.2 * n
    except Exception:
        pass
    return 300.0


HOP = 150.0


class Prog:
    def __init__(self):
        self.ops = []
        self.last_w = {}
        self.readers = {}
        self.fin = []
        self.efree = {}

    @staticmethod
    def _norm(k):
        if k.startswith("ps") and "." in k:
            return k.split(".")[0]
        return k

    _cap = None

    def begin(self):
        self._cap = []

    def end(self):
        l = self._cap
        self._cap = None
        return l

    def replay_merged(self, lists):
        units = []
        for li, l in enumerate(lists):
            n = len(l)
            k = 0
            while k < n:
                k2 = k + 1
                if l[k][0] == "pe":
                    while k2 < n and l[k2][0] == "pe":
                        k2 += 1
                units.append(((k + 0.5) / n, li, k, k2))
                k = k2
        units.sort(key=lambda x: (x[0], x[1]))
        maps = [dict() for _ in lists]
        for _, li, k, k2 in units:
            for kk in range(k, k2):
                eng, fn, reads, writes, chan, force, cost = lists[li][kk]
                g = self.add(eng, fn, reads, writes, chan, force=[maps[li][f] for f in force], cost=cost)
                maps[li][kk] = g

    def _deps(self, eng, reads, writes, force):
        deps = set()
        for r in reads:
            if r in self.last_w:
                deps.add(self.last_w[r])
            if r.startswith("ps"):
                for rd in self.readers.get(r, ()):
                    if self.ops[rd]["eng"] != eng:
                        deps.add(rd)
        for w in writes:
            if w in self.last_w:
                deps.add(self.last_w[w])
            for rd in self.readers.get(w, ()):
                deps.add(rd)
        deps.update(force)
        return deps

    def _est_start(self, eng, deps):
        ready = 0.0
        for d in deps:
            ready = max(ready, self.fin[d] + HOP)
        return max(self.efree.get(eng, 0.0), ready)

    def replay_scheduled(self, lists):
        ptr = [0] * len(lists)
        maps = [dict() for _ in lists]
        while True:
            best = None
            for li, l in enumerate(lists):
                if ptr[li] >= len(l):
                    continue
                eng, fn, reads, writes, chan, force, cost = l[ptr[li]]
                reads = [self._norm(k) for k in reads]
                writes = [self._norm(k) for k in writes]
                deps = self._deps(eng, reads, writes, [maps[li][f] for f in force])
                st = self._est_start(eng, deps)
                if best is None or st < best[0] - 1e-6:
                    best = (st, li)
            if best is None:
                break
            li = best[1]
            while True:
                eng, fn, reads, writes, chan, force, cost = lists[li][ptr[li]]
                g = self.add(eng, fn, reads, writes, chan, force=[maps[li][f] for f in force], cost=cost)
                maps[li][ptr[li]] = g
                ptr[li] += 1
                if eng != "pe" or ptr[li] >= len(lists[li]) or lists[li][ptr[li]][0] != "pe":
                    break

    def schedule_all(self, lists, prereqs, prio):
        ptr = {k: 0 for k in lists}
        maps = {k: dict() for k in lists}
        done = set(k for k in lists if len(lists[k]) == 0)
        order = sorted(lists.keys(), key=lambda k: prio[k])
        remaining = [k for k in order if k not in done]
        while remaining:
            best = None
            for k in remaining:
                if not prereqs.get(k, set()) <= done:
                    continue
                eng, fn, reads, writes, chan, force, cost = lists[k][ptr[k]]
                reads = [self._norm(x) for x in reads]
                writes = [self._norm(x) for x in writes]
                deps = self._deps(eng, reads, writes, [maps[k][f] for f in force])
                st = self._est_start(eng, deps)
                if best is None or st < best[0] - 1e-6:
                    best = (st, k)
            assert best is not None, "scheduler deadlock: prerequisites unsatisfiable"
            k = best[1]
            while True:
                eng, fn, reads, writes, chan, force, cost = lists[k][ptr[k]]
                g = self.add(eng, fn, reads, writes, chan, force=[maps[k][f] for f in force], cost=cost)
                maps[k][ptr[k]] = g
                ptr[k] += 1
                if ptr[k] >= len(lists[k]):
                    done.add(k)
                    remaining.remove(k)
                    break
                if eng != "pe" or lists[k][ptr[k]][0] != "pe":
                    break

    def add(self, eng, fn, reads=(), writes=(), chan=None, force=(), cost=None):
        if cost is None:
            cost = est_cost(eng, fn)
        if self._cap is not None:
            self._cap.append((eng, fn, list(reads), list(writes), chan, list(force), cost))
            return len(self._cap) - 1
        idx = len(self.ops)
        reads = [self._norm(k) for k in reads]
        writes = [self._norm(k) for k in writes]
        deps = self._deps(eng, reads, writes, force)
        deps.discard(idx)
        st = self._est_start(eng, deps)
        if chan is not None:
            self.efree[eng] = st + 60.0
            self.fin.append(st + cost)
        else:
            self.efree[eng] = st + cost
            self.fin.append(st + cost)
        self.ops.append(dict(eng=eng, fn=fn, deps=sorted(deps), chan=chan, force=set(force)))
        for r in reads:
            self.readers.setdefault(r, []).append(idx)
        for w in writes:
            self.last_w[w] = idx
            self.readers[w] = []
        return idx

    def emit(self, nc, stack):
        if MAXOPS is not None and MAXOPS < len(self.ops):
            self.ops = self.ops[:MAXOPS]
            dmas = [i for i, o in enumerate(self.ops) if o["chan"] is not None]
            self.ops.append(dict(eng="sp", fn=None, deps=dmas, chan=None, force=set()))
        ops = self.ops
        for i in SKIPOPS:
            if i < len(ops):
                ops[i]["fn"] = None if ops[i]["chan"] is None else ops[i]["fn"]
        needs = [False] * len(ops)
        for i, o in enumerate(ops):
            for d in o["deps"]:
                p = ops[d]
                if p["chan"] is not None:
                    needs[d] = True
                elif d in o.get("force", ()):
                    needs[d] = True
                elif p["eng"] == o["eng"] and o["chan"] is None and p["eng"] in ("pe", "sp"):
                    continue
                else:
                    needs[d] = True
        sems = {}

        def sem(name):
            if name not in sems:
                sems[name] = stack.enter_context(nc.semaphore(name))
            return sems[name]

        cnt = {}
        val = [None] * len(ops)
        for i, o in enumerate(ops):
            if o["chan"] is not None:
                nm = "c_" + o["chan"]
                cnt[nm] = cnt.get(nm, 0) + (1 if o["chan"].startswith("cc") else 16)
                val[i] = (nm, cnt[nm])
            elif needs[i]:
                nm = "e_" + o["eng"]
                cnt[nm] = cnt.get(nm, 0) + 1
                val[i] = (nm, cnt[nm])
        for nm in cnt:
            sem(nm)
        per_eng = {}
        for i, o in enumerate(ops):
            per_eng.setdefault(o["eng"], []).append(i)
        block = stack.enter_context(nc.Block())

        def make(engname):
            def body(e):
                seen = {}
                for i in per_eng.get(engname, []):
                    o = ops[i]
                    for d in o["deps"]:
                        p = ops[d]
                        if val[d] is None:
                            continue
                        if (p["chan"] is None and p["eng"] == engname and o["chan"] is None
                                and engname in ("pe", "sp") and d not in o.get("force", ())):
                            continue
                        nm, v = val[d]
                        if seen.get(nm, 0) >= v:
                            continue
                        seen[nm] = v
                        e.wait_ge(sems[nm], v)
                    if o["fn"] is None:
                        continue
                    ins = o["fn"](e)
                    if val[i] is not None:
                        nm, v = val[i]
                        ins.then_inc(sems[nm], 16 if (o["chan"] is not None and not o["chan"].startswith("cc")) else 1)
            return body

        block.sync(make("sp"))
        block.tensor(make("pe"))
        block.scalar(make("act"))
        block.vector(make("dve"))
        block.gpsimd(make("pool"))


def build_program():
    nc = bass.Bass("TRN2", target_bir_lowering=False)
    stack = ExitStack()
    P = Prog()

    def din(name, shape, dt=F32):
        return nc.dram_tensor(name, list(shape), dt, kind="ExternalInput").ap()

    def dout(name, shape, dt=F32):
        return nc.dram_tensor(name, list(shape), dt, kind="ExternalOutput").ap()

    xp_d = din("xp", [NPRE * 128, D])
    xm_d = din("xm", [NMAIN * 128, D])
    xs_d = din("xs", [128, D])
    ck_d = din("ck", [4, 128, 128])
    cv_d = din("cv", [4, 128, 128])
    st_d = din("st", [4, 4, 64, 128])
    win_d = din("w_in", [D, DIN])
    wout_d = din("w_out", [D, D])
    npw_d = din("npw", [128, 8])
    wlr_d = din("w_lr", [128, 128])
    sink_d = din("sinks", [8])
    wup_d = din("w_up", [16, 256])
    bgk_d = din("b_gk", [256])
    gnw_d = din("gnw", [128])
    wpost_d = din("wpost", [D])
    ident_d = din("ident", [128, 128], BF)
    masks_d = din("masks", [128, 512])
    blockm_d = din("blockm", [128, 128])
    rowm_d = din("rowm", [128, 4])
    t0b_d = din("t0bias", [128, 1])
    sflag_d = din("sflag", [128, 1])
    cc_in = nc.dram_tensor("cc_in", [128, 256], F32, kind="Internal")
    cc_out = nc.dram_tensor("cc_out", [256, 256], F32, kind="Internal")
    rot_d = din("rot", [128, (NMAIN + 2) * 32])

    ym_d = dout("y_m", [NMAIN * 128, D])
    ys_d = dout("y_s", [128, D])
    kvp_d = dout("kv_p", [128, 256])
    kvs_d = dout("kv_s", [128, 256])
    nsp_d = dout("ns_p", [4, 64, 128])
    nss_d = dout("ns_s", [4, 4, 64, 128])

    def T(name, free, dt=F32, parts=128):
        return stack.enter_context(nc.sbuf_tensor(name, [parts, free], dt))

    def PS(name, free, dt=F32):
        return stack.enter_context(nc.psum_tensor(name, [128, free], dt))

    def V(t, p0, npart, off, dims):
        Fr = 1
        for s in t.shape[1:]:
            Fr *= s
        return bass.AP(th(t), p0 * Fr + off, [[Fr, npart]] + [list(d) for d in dims])

    w_in_bf = T("w_in_bf", 8 * DIN, BF)
    w_out_bf = T("w_out_bf", 8 * D, BF)
    NWS = 4
    wstage = [T(f"wstage{i}", 776) for i in range(NWS)]
    wlr_f = T("wlr_f", 128)
    xs = [T(f"xs{i}", D) for i in range(5)]
    xh = T("xh", D, BF)
    hT = [T(f"hT{i}", D, BF) for i in range(3)]
    junk = T("junk", D, BF)
    stat = T("stat", 8)
    tm = T("tm", 1152, BF)
    fT = [T(f"fT{i}", 9 * 128, BF) for i in range(2)]
    kT = [T(f"kT{i}", 128, BF) for i in range(3)]
    vaug = [T(f"vaug{i}", 130, BF) for i in range(3)]
    pT = T("pT", 2 * 2 * 512, BF)
    gate = [T(f"gate{i}", D, BF) for i in range(2)]
    sgt = T("sgt", 512)
    sgt2 = T("sgt2", 512)
    e1 = T("e1", 256)
    lsp = T("lsp", 256)
    Eb = [T(f"Eb{i}", 256) for i in range(2)]
    Einv = [T(f"Einv{i}", 256) for i in range(2)]
    ec = [T(f"ec{i}", 256) for i in range(2)]
    dec = [T(f"dec{i}", 8) for i in range(3)]
    kp_bf = [T(f"kp_bf{i}", 256, BF) for i in range(2)]
    v_bf = [T(f"v_bf{i}", 512, BF) for i in range(2)]
    AT_bf = T("AT_bf", 512, BF)
    S = T("S", 256)
    Srecv = T("Srecv", 256)
    Dtot = T("Dtot", 2)
    sflag = T("sflag_sb", 1)
    S_bf = T("S_bf", 256, BF)
    gst = T("gst", 16)
    den = T("den", 16)
    mix_tm = T("mix_tm", D, BF)
    mixT = T("mixT", D, BF)
    pst = T("pst", 8)
    tt = T("tt", D)
    ysb = [T(f"ysb{i}", D) for i in range(2)]
    kvout = T("kvout", 256)
    rt1 = T("rt1", 160)
    rt2 = T("rt2", 160)
    glr_tm = T("glr_tm", 16, BF)
    glrT = T("glrT", 128, BF)
    wup_f = T("wup_f", 256)
    wup_bf = T("wup_bf", 256, BF)
    npw_sb = T("npw_sb", 8)
    gnw_sb = T("gnw_sb", 1)
    sk_sb = T("sk_sb", 8)
    es_bc = T("es_bc", 8)
    wpost_bc = T("wpost_bc", D)
    ident = T("ident_sb", 128, BF)
    masks_f = T("masks_f", 4 * 128)
    maskB_bf = T("maskB_bf", 128, BF)
    maskB32_bf = T("maskB32_bf", 128, BF)
    blockm_f = T("blockm_f", 128)
    blockm_bf = T("blockm_bf", 128, BF)
    rowm = T("rowm_sb", 4)
    t0b = T("t0b_sb", 1)
    ones_f = T("ones_f", 1)
    rot = T("rot_sb", (NMAIN + 2) * 32)
    ck_f = T("ck_f", 512)
    cv_f = T("cv_f", 512)
    ck_b = T("ck_b", 512, BF)
    kcT = T("kcT", 512, BF)
    vaug_c = T("vaug_c", 4 * 130, BF)
    Pprev = T("Pprev", 8 * 4 * 128, BF)
    pTo = T("pTo", 1024, BF)
    S0 = T("S0", 4 * 256)
    S0_bf = T("S0_bf", 4 * 256, BF)
    Qs = T("Qs", 2 * 4 * 128, BF)
    kpm = T("kpm", 4 * 256, BF)
    nst = T("nst", 4 * 256)

    psT = PS("psT", 1024, BF)
    psZ = [PS(f"psZ{i}", 512) for i in range(2)]
    psZb = [psZ[i].bitcast(BF) for i in range(2)]
    psG = PS("psG", 512)
    psE = PS("psE", 512)
    psE_bf = psE.bitcast(BF)
    psBk = [PS(f"psB{i}", 512) for i in range(3)]
    psB0b = psBk[0].bitcast(BF)

    def dma(out_ap, in_ap, reads, writes, chan, slow=False):
        if slow:
            P.add("sp", lambda e, o=out_ap, i=in_ap: e.dma_start(out=o, in_=i, allow_slow_non_contiguous=True),
                  reads=reads, writes=writes, chan=chan)
        else:
            P.add("sp", lambda e, o=out_ap, i=in_ap: e.dma_start(out=o, in_=i), reads=reads, writes=writes, chan=chan)

    dma(ident[:], ident_d, [], ["ident"], "ident")
    dma(masks_f[:], masks_d, [], ["masks_f"], "masks")
    dma(blockm_f[:], blockm_d, [], ["blockm_f"], "blockm")
    dma(rowm[:], rowm_d, [], ["rowm"], "rowm")
    dma(t0b[:], t0b_d, [], ["t0b"], "t0b")
    dma(sflag[:], sflag_d, [], ["sflag"], "sflag")
    dma(rot[:], rot_d, [], ["rot"], "rot")
    dma(npw_sb[:], npw_d, [], ["npw"], "npw")
    dma(gnw_sb[:], bass.AP(gnw_d.tensor, 0, [[1, 128], [1, 1]]), [], ["gnw"], "gnw")
    dma(sk_sb[:], bass.AP(sink_d.tensor, 0, [[0, 128], [1, 8]]), [], ["sk"], "sk")
    dma(wpost_bc[:], bass.AP(wpost_d.tensor, 0, [[0, 128], [1, D]]), [], ["wpost"], "wpost")
    dma(wup_f[0:16, :], wup_d, [], ["wup_f.a"], "wupa")
    dma(wup_f[16:17, :], bass.AP(bgk_d.tensor, 0, [[256, 1], [1, 256]]), [], ["wup_f.b"], "wupb")

    P.add("dve", lambda e: e.tensor_copy(wup_bf[0:17, :], wup_f[0:17, :]), ["wup_f.a", "wup_f.b"], ["wup_bf"])
    P.add("dve", lambda e: e.tensor_copy(maskB_bf[:], masks_f[:, 0:128]), ["masks_f"], ["maskB_bf"])
    P.add("dve", lambda e: e.tensor_copy(maskB32_bf[:], masks_f[:, 256:384]), ["masks_f"], ["maskB32_bf"])
    P.add("dve", lambda e: e.tensor_copy(blockm_bf[:], blockm_f[:]), ["blockm_f"], ["blockm_bf"])
    P.add("pool", lambda e: e.memset(glrT[0:32, :], 1.0), [], ["glrT"])
    P.add("pool", lambda e: e.memset(vaug[0][:], 1.0), [], ["vaug0"])
    P.add("pool", lambda e: e.memset(vaug[1][:], 1.0), [], ["vaug1"])
    P.add("pool", lambda e: e.memset(vaug[2][:], 1.0), [], ["vaug2"])
    P.add("pool", lambda e: e.memset(vaug_c[:], 1.0), [], ["vaug_c"])
    P.add("pool", lambda e: e.memset(ones_f[:], 1.0), [], ["ones_f"])
    P.add("pool", lambda e: e.memset(S[:], 0.0), [], ["S0k", "S1k"])
    P.add("pool", lambda e: e.memset(Dtot[:], 1.0), [], ["Dtot"])
    P.add("pool", lambda e: e.memset(S_bf[:], 0.0), [], ["S_bf"])
    P.add("pool", lambda e: e.memset(Pprev[:], 0.0), [], ["Pprev"])
    P.add("pool", lambda e: e.memset(pT[:], 0.0), [], ["pT00", "pT01", "pT10", "pT11"])
    P.add("pool", lambda e: e.memset(Qs[:], 0.0), [], ["Qs"])
    P.add("act", lambda e: e.activation(es_bc[:], sk_sb[:], AF.Exp), ["sk"], ["es_bc"])

    wcount = [0]

    def cast_scaled(eng, out_ap, in_ap, scal_ap, reads, writes):
        if eng == "act":
            if scal_ap is None:
                P.add("act", lambda e: e.copy(out_ap, in_ap), reads, writes)
            else:
                P.add("act", lambda e: e.activation(out_ap, in_ap, AF.Copy, scale=scal_ap), reads, writes)
        else:
            if scal_ap is None:
                P.add("dve", lambda e: e.tensor_copy(out_ap, in_ap), reads, writes)
            else:
                P.add("dve", lambda e: e.tensor_scalar(out_ap, in_ap, scal_ap, None, ALU.mult), reads, writes)

    def load_w_in(k, c0, c1):
        i = wcount[0] % NWS
        wcount[0] += 1
        n = c1 - c0
        assert n <= 776
        dma(wstage[i][:, 0:n], win_d[k * 128:(k + 1) * 128, c0:c1], [], [f"wstage{i}"], f"wstage{i}")
        eng = "dve" if (wcount[0] % 2 == 0) else "act"
        cast_scaled(eng, w_in_bf[:, k * DIN + c0:k * DIN + c0 + n], wstage[i][:, 0:n], npw_sb[:, k:k + 1],
                    [f"wstage{i}", "npw"], [f"w_in.{k}.{c0}"])

    def load_w_out(k, half):
        i = wcount[0] % NWS
        wcount[0] += 1
        c0 = half * 512
        dma(wstage[i][:, 0:512], wout_d[k * 128:(k + 1) * 128, c0:c0 + 512], [], [f"wstage{i}"], f"wstage{i}")
        eng = "dve" if (wcount[0] % 2 == 0) else "act"
        cast_scaled(eng, w_out_bf[:, k * D + c0:k * D + c0 + 512], wstage[i][:, 0:512], gnw_sb[:, 0:1] if k >= 4 else None,
                    [f"wstage{i}", "gnw"], [f"w_out.{k}.{half}"])

    def wkeys(c0):
        return [f"w_in.{k}.{c0}" for k in range(8)]

    W_GKGV = wkeys(CGK)
    W_GQ = wkeys(CGQ)
    W_GG = wkeys(CGG)
    W_LR = ["w_in.lr"]
    WINB = wkeys(0) + wkeys(640)

    def load_w_lr():
        dma(wlr_f[:], wlr_d, [], ["wlr_f"], "wlr")
        P.add("dve", lambda e: e.tensor_tensor(V(w_in_bf, 0, 128, CLR, [[DIN, 8], [1, 16]]), V(wlr_f, 0, 128, 0, [[16, 8], [1, 16]]),
                                               V(npw_sb, 0, 128, 0, [[1, 8], [0, 16]]), ALU.mult), ["wlr_f", "npw"], ["w_in.lr"])
    WOUT = [f"w_out.{k}.{half}" for k in range(8) for half in range(2)]

    NT = NPRE + NMAIN + 1

    def x_src(i):
        if i < NPRE:
            return xp_d[i * 128:(i + 1) * 128, :]
        if i < NPRE + NMAIN:
            j = i - NPRE
            return xm_d[j * 128:(j + 1) * 128, :]
        return xs_d

    def load_x(i):
        slot = i % 5
        dma(xs[slot][:], x_src(i), [], [f"xs{slot}"], f"xs{slot}")

    def front(i):
        slot, p = i % 5, i % 3
        X = f"xs{slot}"
        hTp = hT[p]
        P.add("act", lambda e: e.activation(junk[:], xs[slot][:], AF.Square, accum_out=stat[:, 0:1]), [X], ["stat0"])
        P.add("act", lambda e: e.activation(stat[:, 1:2], stat[:, 0:1], AF.Ln, scale=1.0 / D, bias=EPS), ["stat0"], ["stat1"])
        P.add("act", lambda e: e.activation(stat[:, 2:3], stat[:, 1:2], AF.Exp, scale=-0.5), ["stat1"], ["stat2"])
        P.add("dve", lambda e: e.tensor_scalar(xh[:], xs[slot][:], stat[:, 2:3], None, ALU.mult), [X, "stat2"], ["xh"])
        for k in range(8):
            P.add("pe", lambda e, k=k: e.transpose(psT[:, k * 128:(k + 1) * 128], xh[:, k * 128:(k + 1) * 128], ident[:]),
                  ["xh", "ident"], ["psT"])
        P.add("act", lambda e: e.copy(hTp[:, 0:512], psT[:, 0:512]), ["psT"], [f"hT{p}.a"])
        P.add("dve", lambda e: e.tensor_copy(hTp[:, 512:1024], psT[:, 512:1024]), ["psT"], [f"hT{p}.b"])

    def inproj(i, c0, c1, zi, wkeys):
        n = c1 - c0
        p = i % 3
        hTp = hT[p]
        for k in range(8):
            P.add("pe", lambda e, k=k: e.matmul(psZ[zi][:, 0:n], lhsT=hTp[:, k * 128:(k + 1) * 128],
                                                 rhs=w_in_bf[:, k * DIN + c0:k * DIN + c1],
                                                 start=(k == 0), stop=(k == 7)),
                  [f"hT{p}.a", f"hT{p}.b"] + wkeys, [f"psZ{zi}"])

    def gates_front(i, mB, mC, sample):
        p, p3 = i % 2, i % 3
        hTp, Ebp, Einvp, ecp, decp = hT[p3], Eb[p], Einv[p], ec[p], dec[p3]
        for k in range(8):
            P.add("pe", lambda e, k=k: e.matmul(psE[:, 0:16], lhsT=hTp[:, k * 128:(k + 1) * 128],
                                                 rhs=w_in_bf[:, k * DIN + CLR:k * DIN + CLR + 16], start=(k == 0), stop=(k == 7)),
                  [f"hT{p3}.a", f"hT{p3}.b"] + W_LR, ["psE"])
        P.add("act", lambda e: e.copy(glr_tm[:, 0:16], psE[:, 0:16]), ["psE"], ["glr_tm"])
        P.add("pe", lambda e: e.transpose(psE_bf[0:16, 384:512], glr_tm[:, 0:16], ident[:]), ["glr_tm", "ident"], ["psE"])
        P.add("act", lambda e: e.copy(glrT[0:16, :], psE_bf[0:16, 384:512]), ["psE"], ["glrT"])
        P.add("pe", lambda e: e.matmul(psE[:, 256:512], lhsT=glrT[0:17, :], rhs=wup_bf[0:17, :], start=True, stop=True),
              ["glrT", "wup_bf"], ["psE"])
        P.add("act", lambda e: e.activation(e1[:], psE[:, 256:512], AF.Exp, scale=-1.0), ["psE"], ["e1"])
        P.add("act", lambda e: e.activation(lsp[:], e1[:], AF.Ln, bias=1.0), ["e1"], ["lsp"])
        if mB is not None:
            P.add("pe", lambda e: e.matmul(psG[:, 0:256], lhsT=masks_f[:, mB * 128:(mB + 1) * 128], rhs=lsp[:],
                                           start=True, stop=True), ["lsp", "masks_f"], ["psG"])
        P.add("pe", lambda e: e.matmul(psG[:, 256:512], lhsT=masks_f[:, mC * 128:(mC + 1) * 128], rhs=lsp[:],
                                       start=True, stop=True), ["lsp", "masks_f"], ["psG"])
        if not sample:
            for j in range(2):
                P.add("pe", lambda e, j=j: e.matmul(psE[:, 128 + j:129 + j], lhsT=lsp[:, j * 128:(j + 1) * 128],
                                                     rhs=ones_f[:, 0:1], start=True, stop=True), ["lsp", "ones_f"], ["psE"])
            P.add("act", lambda e: e.activation(decp[:, 0:2], psE[:, 128:130], AF.Exp, scale=-1.0 / 16), ["psE"], [f"dec{p3}"])
        else:
            for j in range(2):
                P.add("pe", lambda e, j=j: e.matmul(psE[:, 128 + 4 * j:132 + 4 * j], lhsT=lsp[:, j * 128:(j + 1) * 128],
                                                     rhs=rowm[:, 0:4], start=True, stop=True), ["lsp", "rowm"], ["psE"])
            P.add("act", lambda e: e.activation(decp[:, 0:8], psE[:, 128:136], AF.Exp, scale=-1.0 / 16), ["psE"], [f"dec{p3}"])
        if mB is not None:
            P.add("act", lambda e: e.activation(Ebp[:], psG[:, 0:256], AF.Exp, scale=-1.0 / 16, bias=LN8), ["psG"], [f"Eb{p}"])
            P.add("act", lambda e: e.activation(Einvp[:], psG[:, 0:256], AF.Exp, scale=1.0 / 16), ["psG"], [f"Einv{p}"])
        P.add("act", lambda e: e.activation(ecp[:], psG[:, 256:512], AF.Exp, scale=-1.0 / 16), ["psG"], [f"ec{p}"])

    def silu_gate(i, zi, g0):
        p = i % 2
        gp = gate[p]
        Z = f"psZ{zi}"
        P.add("act", lambda e: e.activation(sgt[:], psZ[zi][:], AF.Exp, scale=-1.0), [Z], ["sgt"])
        P.add("act", lambda e: e.activation(sgt2[:], sgt[:], AF.Ln, bias=1.0), ["sgt"], ["sgt2"])
        P.add("act", lambda e: e.activation(sgt[:], sgt2[:], AF.Exp, scale=-1.0), ["sgt2"], ["sgt"])
        P.add("dve", lambda e: e.tensor_tensor(gp[:, g0:g0 + 512], psZ[zi][:], sgt[:], ALU.mult), [Z, "sgt"], [f"gate{p}.{g0}"])

    def rotary(zi, col0, nh, rti, dst_t, dst_off, dst_key):
        Z = f"psZ{zi}"
        src16 = V(psZ[zi], 0, 128, col0, [[64, nh], [1, 16]])
        srcsw = V(psZ[zi], 0, 128, col0 + 8, [[64, nh], [-8, 2], [1, 8]])
        cbc = V(rot, 0, 128, rti * 32, [[0, nh], [1, 16]])
        sbc = V(rot, 0, 128, rti * 32 + 16, [[0, nh], [8, 2], [1, 8]])
        t1 = V(rt1, 0, 128, 0, [[16, nh], [1, 16]])
        t2 = V(rt2, 0, 128, 0, [[16, nh], [8, 2], [1, 8]])
        t2f = V(rt2, 0, 128, 0, [[16, nh], [1, 16]])
        d16 = V(dst_t, 0, 128, dst_off, [[64, nh], [1, 16]])
        d48 = V(dst_t, 0, 128, dst_off + 16, [[64, nh], [1, 48]])
        s48 = V(psZ[zi], 0, 128, col0 + 16, [[64, nh], [1, 48]])
        P.add("dve", lambda e: e.tensor_tensor(t1, src16, cbc, ALU.mult), [Z, "rot"], ["rt1"])
        P.add("dve", lambda e: e.tensor_tensor(t2, srcsw, sbc, ALU.mult), [Z, "rot"], ["rt2"])
        P.add("dve", lambda e: e.tensor_tensor(d16, t1, t2f, ALU.add), ["rt1", "rt2"], [dst_key + ".r"])
        P.add("dve", lambda e: e.tensor_copy(d48, s48), [Z], [dst_key + ".p"])

    def kv_out(zi, dst_d, okey):
        P.add("dve", lambda e: e.tensor_tensor(V(kvout, 0, 128, 0, [[64, 2], [1, 16]]), V(rt1, 0, 128, 0, [[16, 2], [1, 16]]),
                                               V(rt2, 0, 128, 0, [[16, 2], [1, 16]]), ALU.add), ["rt1", "rt2"], ["kvout.a"])
        P.add("dve", lambda e: e.tensor_copy(V(kvout, 0, 128, 16, [[64, 2], [1, 48]]), V(psZ[zi], 0, 128, 16, [[64, 2], [1, 48]])),
              [f"psZ{zi}"], ["kvout.b"])
        P.add("dve", lambda e: e.tensor_copy(kvout[:, 128:256], psZ[zi][:, 128:256]), [f"psZ{zi}"], ["kvout.c"])
        dma(dst_d, kvout[:], ["kvout.a", "kvout.b", "kvout.c"], [okey], "kvout")
        outkeys.append(okey)

    def state_update(i, acc_decay=False):
        p, p3 = i % 2, i % 3
        kpp, vp, decp = kp_bf[p], v_bf[p], dec[p3]
        if acc_decay:
            P.add("dve", lambda e: e.tensor_tensor(Dtot[:, 0:2], Dtot[:, 0:2], decp[:, 0:2], ALU.mult),
                  ["Dtot", f"dec{p3}"], ["Dtot"])
        for h in range(4):
            j, b0 = h // 2, (h % 2) * 64
            P.add("pe", lambda e, h=h, j=j, b0=b0: e.matmul(psBk[2][b0:b0 + 64, j * 128:(j + 1) * 128],
                                                             lhsT=kpp[:, h * 64:(h + 1) * 64],
                                                             rhs=vp[:, h * 128:(h + 1) * 128], start=True, stop=True),
                  [f"kp_bf{p}", f"v_bf{p}"], ["psB2"])
        for j in range(2):
            P.add("dve", lambda e, j=j: e.scalar_tensor_tensor(S[:, j * 128:(j + 1) * 128], S[:, j * 128:(j + 1) * 128],
                                                               decp[:, j:j + 1], psBk[2][:, j * 128:(j + 1) * 128],
                                                               ALU.mult, ALU.add),
                  [f"S{j}k", f"dec{p3}", "psB2"], [f"S{j}k"])
        P.add("dve", lambda e: e.tensor_copy(S_bf[:], S[:]), ["S0k", "S1k"], ["S_bf"])

    def attn_finish(i, g):
        p = i % 2
        gp = gate[p]
        P.add("dve", lambda e: e.tensor_tensor(den[:, g * 4:(g + 1) * 4], V(psBk[2], 0, 128, 64, [[65, 4]]),
                                               es_bc[:, g * 4:(g + 1) * 4], ALU.add), ["psB2", "es_bc"], [f"den{g}"])
        P.add("dve", lambda e: e.reciprocal(den[:, 8 + g * 4:12 + g * 4], den[:, g * 4:(g + 1) * 4]), [f"den{g}"], [f"rec{g}"])
        for b in range(4):
            h = g * 4 + b
            P.add("dve", lambda e, h=h, b=b: e.scalar_tensor_tensor(
                mix_tm[:, h * 64:(h + 1) * 64], psBk[2][:, b * 65:b * 65 + 64], den[:, 8 + h:9 + h],
                gp[:, h * 64:(h + 1) * 64], ALU.mult, ALU.mult),
                ["psB2", f"rec{g}", f"gate{p}.0"], [f"mix.a{h}"])

    def gla_finish(i):
        p = i % 2
        gp = gate[p]
        for h in range(4):
            P.add("act", lambda e, h=h: e.activation(junk[:, h * 128:(h + 1) * 128], psBk[1][:, h * 128:(h + 1) * 128], AF.Square,
                                                     accum_out=gst[:, h:h + 1]), ["psB1"], [f"gst{h}"])
        P.add("act", lambda e: e.activation(gst[:, 4:8], gst[:, 0:4], AF.Ln, scale=1.0 / 128, bias=EPS),
              [f"gst{h}" for h in range(4)], ["gstl"])
        P.add("act", lambda e: e.activation(gst[:, 8:12], gst[:, 4:8], AF.Exp, scale=-0.5), ["gstl"], ["grs"])
        for h in range(4):
            P.add("dve", lambda e, h=h: e.scalar_tensor_tensor(
                mix_tm[:, 512 + h * 128:512 + (h + 1) * 128], psBk[1][:, h * 128:(h + 1) * 128], gst[:, 8 + h:9 + h],
                gp[:, 512 + h * 128:512 + (h + 1) * 128], ALU.mult, ALU.mult),
                ["psB1", "grs", f"gate{p}.512"], [f"mix.g{h}"])

    ycount = [0]

    def out_proj(i, dst_rows, okey):
        xslot = i % 5
        mixkeys = [f"mix.a{h}" for h in range(8)] + [f"mix.g{h}" for h in range(4)]
        for k in range(8):
            P.add("pe", lambda e, k=k: e.transpose(psB0b[:, k * 128:(k + 1) * 128], mix_tm[:, k * 128:(k + 1) * 128], ident[:]),
                  mixkeys + ["ident"], ["psB0"])
        P.add("act", lambda e: e.copy(mixT[:, 0:512], psB0b[:, 0:512]), ["psB0"], ["mixT.a"])
        P.add("dve", lambda e: e.tensor_copy(mixT[:, 512:1024], psB0b[:, 512:1024]), ["psB0"], ["mixT.b"])
        for half in range(2):
            for k in range(8):
                P.add("pe", lambda e, k=k, half=half: e.matmul(psBk[1 + half][:], lhsT=mixT[:, k * 128:(k + 1) * 128],
                                                                rhs=w_out_bf[:, k * D + half * 512:k * D + (half + 1) * 512],
                                                                start=(k == 0), stop=(k == 7)),
                      ["mixT.a", "mixT.b"] + WOUT, [f"psB{1 + half}"])
        for half in range(2):
            P.add("act", lambda e, half=half: e.activation(junk[:, half * 512:(half + 1) * 512], psBk[1 + half][:], AF.Square,
                                                           accum_out=pst[:, half:half + 1]), [f"psB{1 + half}"], [f"pst{half}"])
        P.add("dve", lambda e: e.tensor_tensor(pst[:, 2:3], pst[:, 0:1], pst[:, 1:2], ALU.add), ["pst0", "pst1"], ["pst2"])
        P.add("act", lambda e: e.activation(pst[:, 3:4], pst[:, 2:3], AF.Ln, scale=1.0 / D, bias=EPS), ["pst2"], ["pst3"])
        P.add("act", lambda e: e.activation(pst[:, 4:5], pst[:, 3:4], AF.Exp, scale=-0.5), ["pst3"], ["prs"])
        for half in range(2):
            P.add("dve", lambda e, half=half: e.scalar_tensor_tensor(
                tt[:, half * 512:(half + 1) * 512], psBk[1 + half][:], pst[:, 4:5], wpost_bc[:, half * 512:(half + 1) * 512],
                ALU.mult, ALU.mult), [f"psB{1 + half}", "prs", "wpost"], [f"tt{half}"])
        ys_i = ycount[0] % 2
        ycount[0] += 1
        P.add("pool", lambda e: e.tensor_tensor(ysb[ys_i][:], tt[:], xs[xslot][:], ALU.add),
              ["tt0", "tt1", f"xs{xslot}"], [f"ysb{ys_i}"])
        dma(dst_rows, ysb[ys_i][:], [f"ysb{ys_i}"], [okey], f"ysb{ys_i}")
        outkeys.append(okey)

    outkeys = []
    TMK = {0: ["tm.q.r", "tm.q.p"], 1: ["tm.q.r", "tm.q.p"], 2: ["tm.q.r", "tm.q.p"], 3: ["tm.q.r", "tm.q.p"],
           4: ["tm.k.r", "tm.k.p"], 5: ["tm.qin"], 6: ["tm.qin"], 7: ["tm.kout"], 8: ["tm.kout"]}

    def fT_transposes(i):
        p, p3 = i % 2, i % 3
        fTp, kTp = fT[p], kT[p3]
        for b in range(9):
            dst = psZb[0][:, b * 128:(b + 1) * 128] if b < 8 else psZb[1][:, 0:128]
            P.add("pe", lambda e, b=b, dst=dst: e.transpose(dst, tm[:, b * 128:(b + 1) * 128], ident[:]),
                  TMK[b] + ["ident"], ["psZ0" if b < 8 else "psZ1"])
        P.add("dve", lambda e: e.tensor_copy(fTp[:, 0:512], psZb[0][:, 0:512]), ["psZ0"], [f"fT{p}.q"])
        P.add("act", lambda e: e.copy(kTp[:], psZb[0][:, 512:640]), ["psZ0"], [f"kT{p3}"])
        P.add("act", lambda e: e.copy(fTp[:, 640:1024], psZb[0][:, 640:1024]), ["psZ0"], [f"fT{p}.g"])
        P.add("act", lambda e: e.copy(fTp[:, 1024:1152], psZb[1][:, 0:128]), ["psZ1"], [f"fT{p}.8"])

    def kind(i):
        return "pre" if i < NPRE else ("main" if i < NPRE + NMAIN else "smp")

    def stage_A0(i):
        if i + 1 < NT:
            load_x(i + 1)
        front(i)

    def stage_A1(i):
        k = kind(i)
        if k == "pre":
            gates_front(i, None, 1, False)
        elif k == "main":
            gates_front(i, 0, 1, False)
        else:
            gates_front(i, 2, 3, True)

    def stage_A2(i):
        k = kind(i)
        p, p3 = i % 2, i % 3
        Ebp, Einvp, ecp, kpp, vp, gp = Eb[p], Einv[p], ec[p], kp_bf[p], v_bf[p], gate[p]
        vaugp = vaug[p3]
        if k == "pre":
            inproj(i, CGK, CGK + 256, 0, W_GKGV)
            P.add("dve", lambda e: e.tensor_tensor(kpp[:], psZ[0][:, 0:256], ecp[:], ALU.mult), ["psZ0", f"ec{p}"], [f"kp_bf{p}"])
            inproj(i, CGV, CGV + 512, 1, W_GKGV)
            P.add("act", lambda e: e.copy(vp[:], psZ[1][:]), ["psZ1"], [f"v_bf{p}"])
            if i == NPRE - 1:
                inproj(i, CK, CK + 256, 0, WINB)
                rotary(0, 0, 2, 0, tm, 512, "tm.k")
                P.add("dve", lambda e: e.tensor_copy(V(vaugp, 0, 128, 0, [[65, 2], [1, 64]]),
                                                     V(psZ[0], 0, 128, 128, [[64, 2], [1, 64]])), ["psZ0"], [f"vaug{p3}"])
                kTp = kT[p3]
                P.add("pe", lambda e: e.transpose(psZb[1][:, 0:128], tm[:, 512:640], ident[:]), ["tm.k.r", "tm.k.p", "ident"], ["psZ1"])
                P.add("dve", lambda e: e.tensor_copy(kTp[:], psZb[1][:, 0:128]), ["psZ1"], [f"kT{p3}"])
            return
        t = i - NPRE
        rti = 1 + t if k == "main" else NMAIN + 1
        inproj(i, CGQ, CGQ + 512, 0, W_GQ + W_GKGV)
        P.add("dve", lambda e: e.tensor_tensor(tm[:, 640:896], psZ[0][:, 0:256], Ebp[:], ALU.mult), ["psZ0", f"Eb{p}"], ["tm.qin"])
        P.add("dve", lambda e: e.tensor_tensor(tm[:, 896:1152], psZ[0][:, 256:512], Einvp[:], ALU.mult), ["psZ0", f"Einv{p}"], ["tm.kout"])
        P.add("dve", lambda e: e.tensor_tensor(kpp[:], psZ[0][:, 256:512], ecp[:], ALU.mult), ["psZ0", f"ec{p}"], [f"kp_bf{p}"])
        if k == "smp":
            for s_ in range(4):
                P.add("pool", lambda e, s_=s_: e.tensor_scalar(kpm[:, s_ * 256:(s_ + 1) * 256], kpp[:], rowm[:, s_:s_ + 1], None, ALU.mult),
                      [f"kp_bf{p}", "rowm"], [f"kpm{s_}"])
        inproj(i, CGV, CGV + 512, 1, W_GKGV)
        P.add("act", lambda e: e.copy(vp[:], psZ[1][:]), ["psZ1"], [f"v_bf{p}"])
        inproj(i, CGG, CGG + 512, 0, W_GG)
        silu_gate(i, 0, 512)
        inproj(i, CAG, CAG + 512, 1, WINB)
        silu_gate(i, 1, 0)
        inproj(i, CQ, CQ + 512, 0, WINB)
        rotary(0, 0, 8, rti, tm, 0, "tm.q")
        inproj(i, CK, CK + 256, 1, WINB)
        rotary(1, 0, 2, rti, tm, 512, "tm.k")
        P.add("dve", lambda e: e.tensor_copy(V(vaugp, 0, 128, 0, [[65, 2], [1, 64]]),
                                             V(psZ[1], 0, 128, 128, [[64, 2], [1, 64]])), ["psZ1"], [f"vaug{p3}"])
        if k == "main" and t == NMAIN - 1:
            kv_out(1, kvp_d, "o.kvp")
        if k == "smp":
            kv_out(1, kvs_d, "o.kvs")
        fT_transposes(i)

    def gla_AT(i, mask_bf, mask_key):
        p = i % 2
        fTp = fT[p]
        lastmm = None
        for h in range(4):
            j, b0 = h // 2, (h % 2) * 64
            lastmm = P.add("pe", lambda e, h=h, j=j, b0=b0: e.matmul(psBk[0][:, h * 128:(h + 1) * 128],
                                                                      lhsT=fTp[b0:b0 + 64, (7 + j) * 128:(8 + j) * 128],
                                                                      rhs=fTp[b0:b0 + 64, (5 + j) * 128:(6 + j) * 128], start=True, stop=True),
                           [f"fT{p}.g", f"fT{p}.8"], ["psB0"], force=([lastmm] if lastmm is not None else ()))
        P.add("dve", lambda e: e.tensor_tensor(V(AT_bf, 0, 128, 0, [[128, 4], [1, 128]]), V(psBk[0], 0, 128, 0, [[128, 4], [1, 128]]),
                                               V(mask_bf, 0, 128, 0, [[0, 4], [1, 128]]), ALU.mult),
              ["psB0", mask_key], ["AT_bf"])

    def stage_B(i):
        k = kind(i)
        p, p3 = i % 2, i % 3
        if k == "pre":
            state_update(i, acc_decay=True)
            return
        if k == "smp":
            stage_B_sample(i)
            return
        t = i - NPRE
        fTp, vp = fT[p], v_bf[p]
        cur, prev = i % 3, (i - 1) % 3
        kTc, kTp_, vac, vap = kT[cur], kT[prev], vaug[cur], vaug[prev]
        for g in range(2):
            b0 = g * 64
            for blk, ksl, kt in ((0, prev, kTp_), (1, cur, kTc)):
                P.add("pe", lambda e, b0=b0, blk=blk, kt=kt: e.matmul(
                    psBk[blk][:], lhsT=kt[b0:b0 + 64, :], rhs=fTp[b0:b0 + 64, 0:512], start=True, stop=True),
                    [f"kT{ksl}", f"fT{p}.q"], [f"psB{blk}"])
                bias = t0b[:, 0:1] if (t == 0 and blk == 0) else 0.0
                bias_hi = t0b[64:128, 0:1] if (t == 0 and blk == 0) else 0.0
                base = (g * 2 + blk) * 512
                if blk == 0:
                    P.add("act", lambda e, base=base, bias=bias: e.activation(
                        V(pT, 0, 128, base, [[128, 4], [1, 64]]), V(psBk[0], 0, 128, 0, [[128, 4], [1, 64]]),
                        AF.Exp, scale=0.125, bias=bias), ["psB0", "t0b"], [f"pT{g}0"])
                    P.add("act", lambda e, base=base, bias_hi=bias_hi: e.activation(
                        V(pT, 64, 64, base + 64, [[128, 4], [1, 64]]), V(psBk[0], 64, 64, 64, [[128, 4], [1, 64]]),
                        AF.Exp, scale=0.125, bias=bias_hi), ["psB0", "t0b"], [f"pT{g}0"])
                else:
                    P.add("act", lambda e, base=base: e.activation(
                        pT[0:64, base:base + 512], psBk[1][0:64, :], AF.Exp, scale=0.125), ["psB1"], [f"pT{g}1"])
                    P.add("act", lambda e, base=base: e.activation(
                        V(pT, 64, 64, base + 64, [[128, 4], [1, 64]]), V(psBk[1], 64, 64, 64, [[128, 4], [1, 64]]),
                        AF.Exp, scale=0.125), ["psB1"], [f"pT{g}1"])
            for b in range(4):
                po, oo = (g * 2 + 0) * 512 + b * 128, (g * 2 + 1) * 512 + b * 128
                ocol = b * 65
                vcol = g * 65
                P.add("pe", lambda e, po=po, ocol=ocol, vcol=vcol: e.matmul(
                    psBk[2][:, ocol:ocol + 65], lhsT=pT[:, po:po + 128], rhs=vap[:, vcol:vcol + 65], start=True, stop=False),
                    [f"pT{g}0", f"vaug{prev}"], ["psB2"])
                P.add("pe", lambda e, oo=oo, ocol=ocol, vcol=vcol: e.matmul(
                    psBk[2][:, ocol:ocol + 65], lhsT=pT[:, oo:oo + 128], rhs=vac[:, vcol:vcol + 65], start=False, stop=True),
                    [f"pT{g}1", f"vaug{cur}"], ["psB2"])
            attn_finish(i, g)
        gla_AT(i, maskB_bf, "maskB_bf")
        for h in range(4):
            j, b0 = h // 2, (h % 2) * 64
            P.add("pe", lambda e, h=h, j=j, b0=b0: e.matmul(psBk[1][:, h * 128:(h + 1) * 128],
                                                             lhsT=fTp[b0:b0 + 64, (5 + j) * 128:(6 + j) * 128],
                                                             rhs=S_bf[b0:b0 + 64, j * 128:(j + 1) * 128], start=True, stop=False),
                  [f"fT{p}.g", "S_bf"], ["psB1"])
            P.add("pe", lambda e, h=h: e.matmul(psBk[1][:, h * 128:(h + 1) * 128], lhsT=AT_bf[:, h * 128:(h + 1) * 128],
                                                 rhs=vp[:, h * 128:(h + 1) * 128], start=False, stop=True),
                  ["AT_bf", f"v_bf{p}"], ["psB1"])
        state_update(i)
        gla_finish(i)
        if t == NMAIN - 1:
            dma(bass.AP(nsp_d.tensor, 0, [[128, 128], [2 * 64 * 128, 2], [1, 128]]), S[:], ["S0k", "S1k"], ["o.nsp"], "nsp")
            outkeys.append("o.nsp")
        out_proj(i, ym_d[t * 128:(t + 1) * 128, :], f"o.y{t}")

    PAIRS = [[2 * q, 2 * q + 1] for q in range(NCORES // 2)]

    def exchange():
        dma(cc_in.ap(), S[:], ["S0k", "S1k"], ["cc_in"], "xchg_in")
        P.add("pool", lambda e: e.collective_compute("AllGather", ALU.bypass, replica_groups=PAIRS,
                                                     ins=[cc_in.ap()], outs=[cc_out.ap()]),
              ["cc_in"], ["cc_out"], chan="cc_gla", cost=12000.0)
        dma(Srecv[:], cc_out.ap()[0:128, :], ["cc_out"], ["Srecv"], "srecv")
        for j in range(2):
            P.add("dve", lambda e, j=j: e.scalar_tensor_tensor(S[:, j * 128:(j + 1) * 128], Srecv[:, j * 128:(j + 1) * 128],
                                                               Dtot[:, j:j + 1], S[:, j * 128:(j + 1) * 128],
                                                               ALU.mult, ALU.add),
                  ["Srecv", "Dtot", f"S{j}k"], [f"S{j}k"])
        P.add("dve", lambda e: e.tensor_scalar(S[:], S[:], sflag[:, 0:1], None, ALU.mult), ["S0k", "S1k", "sflag"], ["S0k", "S1k"])
        P.add("dve", lambda e: e.tensor_copy(S_bf[:], S[:]), ["S0k", "S1k"], ["S_bf"])

    def sample_prep():
        dma(V(ck_f, 0, 128, 0, [[128, 4], [1, 128]]), bass.AP(ck_d.tensor, 0, [[128, 128], [128 * 128, 4], [1, 128]]), [], ["ck_f"], "ck")
        dma(V(cv_f, 0, 128, 0, [[128, 4], [1, 128]]), bass.AP(cv_d.tensor, 0, [[128, 128], [128 * 128, 4], [1, 128]]), [], ["cv_f"], "cv")
        for s_ in range(4):
            for j in range(2):
                dma(S0[:, (s_ * 2 + j) * 128:(s_ * 2 + j + 1) * 128],
                    bass.AP(st_d.tensor, s_ * 4 * 64 * 128 + j * 2 * 64 * 128, [[128, 128], [1, 128]]), [], [f"S0.{s_}.{j}"], f"S0_{s_}_{j}")
        P.add("pool", lambda e: e.tensor_copy(S0_bf[:], S0[:]), S0keys, ["S0_bf"])
        P.add("pool", lambda e: e.tensor_copy(ck_b[:], ck_f[:]), ["ck_f"], ["ck_b"])
        P.add("pool", lambda e: e.tensor_copy(V(vaug_c, 0, 128, 0, [[65, 8], [1, 64]]), V(cv_f, 0, 128, 0, [[64, 8], [1, 64]])),
              ["cv_f", "vaug_c"], ["vaug_c"])

    S0keys = [f"S0.{s_}.{j}" for s_ in range(4) for j in range(2)]

    def stage_B_sample(i):
        p, p3 = i % 2, i % 3
        fTp, vp, decp = fT[p], v_bf[p], dec[p3]
        cur = p3
        kTc, vac = kT[cur], vaug[cur]
        for s_ in range(4):
            P.add("pe", lambda e, s_=s_: e.transpose(psB0b[:, s_ * 128:(s_ + 1) * 128], ck_b[:, s_ * 128:(s_ + 1) * 128], ident[:]),
                  ["ck_b", "ident"], ["psB0"])
        P.add("dve", lambda e: e.tensor_copy(kcT[:], psB0b[:, 0:512]), ["psB0"], ["kcT"])
        for g in range(2):
            b0 = g * 64
            for s_ in range(4):
                P.add("pe", lambda e, b0=b0, s_=s_: e.matmul(
                    psBk[0][:, s_ * 128:(s_ + 1) * 128], lhsT=kcT[b0:b0 + 64, s_ * 128:(s_ + 1) * 128],
                    rhs=V(fTp, b0, 64, 32 * s_, [[128, 4], [1, 32]]), start=True, stop=True),
                    ["kcT", f"fT{p}.q"], ["psB0"])
            for s_ in range(4):
                P.add("act", lambda e, g=g, s_=s_: e.activation(
                    V(Pprev, 0, 128, ((g * 4) * 4 + s_) * 128 + 32 * s_, [[512, 4], [1, 32]]),
                    V(psBk[0], 0, 128, s_ * 128, [[32, 4], [1, 32]]), AF.Exp, scale=0.125),
                    ["psB0"], [f"Pprev{g}"])
            P.add("pe", lambda e, b0=b0: e.matmul(psBk[1][:], lhsT=kTc[b0:b0 + 64, :], rhs=fTp[b0:b0 + 64, 0:512],
                                                   start=True, stop=True), [f"kT{cur}", f"fT{p}.q"], ["psB1"])
            P.add("act", lambda e, g=g: e.activation(pTo[:, g * 512:(g + 1) * 512], psBk[1][:], AF.Exp, scale=0.125),
                  ["psB1"], [f"pTo{g}"])
            P.add("dve", lambda e, g=g: e.tensor_tensor(V(pT, 0, 128, (g * 2 + 1) * 512, [[128, 4], [1, 128]]),
                                                        V(pTo, 0, 128, g * 512, [[128, 4], [1, 128]]),
                                                        V(blockm_bf, 0, 128, 0, [[0, 4], [1, 128]]), ALU.mult),
                  [f"pTo{g}", "blockm_bf"], [f"pT{g}1"])
            for b in range(4):
                h = g * 4 + b
                ocol = b * 65
                for s_ in range(4):
                    P.add("pe", lambda e, g=g, h=h, s_=s_, ocol=ocol: e.matmul(
                        psBk[2][:, ocol:ocol + 65], lhsT=Pprev[:, (h * 4 + s_) * 128:(h * 4 + s_ + 1) * 128],
                        rhs=vaug_c[:, (s_ * 2 + g) * 65:(s_ * 2 + g + 1) * 65], start=(s_ == 0), stop=False),
                        [f"Pprev{g}", "vaug_c"], ["psB2"])
                oo = (g * 2 + 1) * 512 + b * 128
                P.add("pe", lambda e, g=g, oo=oo, ocol=ocol: e.matmul(
                    psBk[2][:, ocol:ocol + 65], lhsT=pT[:, oo:oo + 128], rhs=vac[:, g * 65:(g + 1) * 65],
                    start=False, stop=True), [f"pT{g}1", f"vaug{cur}"], ["psB2"])
            attn_finish(i, g)
        gla_AT(i, maskB32_bf, "maskB32_bf")
        for j in range(2):
            P.add("pool", lambda e, j=j: e.tensor_copy(V(Qs, 0, 128, j * 512, [[160, 4], [1, 32]]),
                                                       V(fTp, 0, 128, (5 + j) * 128, [[32, 4], [1, 32]])), [f"fT{p}.g", "Qs"], ["Qs"])
        for h in range(4):
            j, b0 = h // 2, (h % 2) * 64
            for s_ in range(4):
                P.add("pe", lambda e, h=h, j=j, b0=b0, s_=s_: e.matmul(
                    psBk[1][:, h * 128:(h + 1) * 128], lhsT=Qs[b0:b0 + 64, (j * 4 + s_) * 128:(j * 4 + s_ + 1) * 128],
                    rhs=S0_bf[b0:b0 + 64, (s_ * 2 + j) * 128:(s_ * 2 + j + 1) * 128], start=(s_ == 0), stop=False),
                    ["Qs", "S0_bf"], ["psB1"])
            P.add("pe", lambda e, h=h: e.matmul(psBk[1][:, h * 128:(h + 1) * 128], lhsT=AT_bf[:, h * 128:(h + 1) * 128],
                                                 rhs=vp[:, h * 128:(h + 1) * 128], start=False, stop=True),
                  ["AT_bf", f"v_bf{p}"], ["psB1"])
        gla_finish(i)
        psU = [psBk[0], psBk[2]]
        for s_ in range(4):
            for h in range(4):
                j, b0 = h // 2, (h % 2) * 64
                col = ((s_ % 2) * 2 + j) * 128
                P.add("pe", lambda e, s_=s_, h=h, b0=b0, col=col: e.matmul(
                    psU[s_ // 2][b0:b0 + 64, col:col + 128], lhsT=kpm[:, s_ * 256 + h * 64:s_ * 256 + (h + 1) * 64],
                    rhs=vp[:, h * 128:(h + 1) * 128], start=True, stop=True),
                    [f"kpm{s_}", f"v_bf{p}"], [["psB0"], ["psB2"]][s_ // 2])
        for s_ in range(4):
            for j in range(2):
                col = ((s_ % 2) * 2 + j) * 128
                P.add("dve", lambda e, s_=s_, j=j, col=col: e.scalar_tensor_tensor(
                    nst[:, (s_ * 2 + j) * 128:(s_ * 2 + j + 1) * 128], S0[:, (s_ * 2 + j) * 128:(s_ * 2 + j + 1) * 128],
                    decp[:, 4 * j + s_:4 * j + s_ + 1], psU[s_ // 2][:, col:col + 128], ALU.mult, ALU.add),
                    S0keys + [f"dec{p3}"] + [["psB0"], ["psB2"]][s_ // 2], [f"nst{s_}{j}"])
                dma(bass.AP(nss_d.tensor, s_ * 4 * 64 * 128 + j * 2 * 64 * 128, [[128, 128], [1, 128]]),
                    nst[:, (s_ * 2 + j) * 128:(s_ * 2 + j + 1) * 128], [f"nst{s_}{j}"], [f"o.nss{s_}{j}"], f"nss{s_}{j}")
                outkeys.append(f"o.nss{s_}{j}")
        out_proj(i, ys_d, "o.ys")

    load_x(0)
    load_w_lr()
    wjobs = []
    for k in range(8):
        wjobs.append((load_w_in, (k, CGK, CGG)))
    for k in range(8):
        wjobs.append((load_w_in, (k, CGQ, CGK)))
    for k in range(8):
        wjobs.append((load_w_in, (k, CGG, CLR)))
    for k in range(8):
        wjobs.append((load_w_in, (k, 0, 640)))
        wjobs.append((load_w_in, (k, 640, 1280)))
    for k in range(8):
        wjobs.append((load_w_out, (k, 0)))
        wjobs.append((load_w_out, (k, 1)))
    wjobs.append((sample_prep, ()))
    WPER = max(4, -(-len(wjobs) // max(1, min(NPRE, 11))))
    segW = []
    for j0 in range(0, len(wjobs), WPER):
        P.begin()
        for fn_, args_ in wjobs[j0:j0 + WPER]:
            fn_(*args_)
        segW.append(P.end())

    def capture(fn, i):
        P.begin()
        fn(i)
        return P.end()

    segA0 = [capture(stage_A0, i) for i in range(NT)]
    segA1 = [capture(stage_A1, i) for i in range(NT)]
    segA2 = [capture(stage_A2, i) for i in range(NT)]
    segB = [capture(stage_B, i) for i in range(NT)]
    if PIPELINE and SCHED == 2:
        L, PR, PRIO = {}, {}, {}
        nW = len(segW)
        for w in range(nW):
            L[("W", w)] = segW[w]
            PR[("W", w)] = {("W", w - 1)} if w > 0 else set()
            PRIO[("W", w)] = (w, 4)
        for i in range(NT):
            for st_, seg, off, rank in (("A0", segA0, 0, 3), ("A1", segA1, 1, 2), ("A2", segA2, 2, 1), ("B", segB, 3, 0)):
                L[(st_, i)] = seg[i]
                PRIO[(st_, i)] = (i + off, rank)
            pa0 = {("A0", i - 1), ("B", i - 4), ("A2", i - 3)}
            pa1 = {("A0", i), ("A1", i - 1), ("A2", i - 2), ("B", i - 3)}
            pa2 = {("A1", i), ("A2", i - 1), ("B", i - 2), ("W", min(i + 1, nW - 1))}
            pb = {("A2", i), ("B", i - 1)}
            for key, pre in ((("A0", i), pa0), (("A1", i), pa1), (("A2", i), pa2), (("B", i), pb)):
                PR[key] = set(p for p in pre if p in L or (p[0] != "W" and 0 <= p[1] < NT) or (p[0] == "W" and 0 <= p[1] < nW))
        for key in PR:
            PR[key] = set(p for p in PR[key] if (p[0] == "W" and 0 <= p[1] < nW) or (p[0] != "W" and 0 <= p[1] < NT))
        P.begin()
        exchange()
        L[("X", 0)] = P.end()
        PR[("X", 0)] = {("B", NPRE - 1)}
        PRIO[("X", 0)] = (NPRE - 1 + 3, -1)
        PR[("B", NPRE)].add(("X", 0))
        P.schedule_all(L, PR, PRIO)
    else:
        for s_ in range(NT + 3):
            lists = []
            if PIPELINE:
                if 0 <= s_ - 3 < NT:
                    lists.append(segB[s_ - 3])
                if 0 <= s_ - 2 < NT:
                    lists.append(segA2[s_ - 2])
                if 0 <= s_ - 1 < NT:
                    lists.append(segA1[s_ - 1])
                if s_ < NT:
                    lists.append(segA0[s_])
                if s_ < len(segW):
                    lists.append(segW[s_])
                if SCHED:
                    P.replay_scheduled(lists)
                else:
                    P.replay_merged(lists)
            else:
                if s_ < len(segW):
                    P.replay_merged([segW[s_]])
                if s_ < NT:
                    P.replay_merged([segA0[s_]])
                    P.replay_merged([segA1[s_]])
                    P.replay_merged([segA2[s_]])
                    P.replay_merged([segB[s_]])
    P.add("sp", None, outkeys, [])
    P.emit(nc, stack)
    stack.close()
    return nc


_CACHE = {}


def _rot_table(pos):
    half = 8
    inv = np.float32(500000.0) ** (-np.arange(half, dtype=np.float32) * np.float32(2.0 / 16))
    ang = pos.astype(np.float32)[:, None] * inv[None, :]
    c = np.cos(ang.astype(np.float64)).astype(np.float32)
    s = np.sin(ang.astype(np.float64)).astype(np.float32)
    return np.concatenate([c, c, -s, s], axis=1)


def kernel(x_prompt, x_sample, cache_k, cache_v, state_gla, norm_pre_w, w_in, attn_sinks, w_gk_up, b_gk,
           gla_norm_w, w_out, norm_post_w):
    f32 = np.float32
    x_prompt = np.asarray(x_prompt, f32)
    x_sample = np.asarray(x_sample, f32)
    if "nc" not in _CACHE:
        _CACHE["nc"] = build_program()
    nc = _CACHE["nc"]
    w_in0 = np.asarray(w_in, f32)[0]
    qperm = np.concatenate([np.arange(h * 64, (h + 1) * 64) for h in (0, 4, 1, 5, 2, 6, 3, 7)])
    perm = np.concatenate([qperm, np.arange(512, DIN)])
    w_in_p = np.ascontiguousarray(w_in0[:, perm])
    ident = np.eye(128, dtype=f32).astype(ml_dtypes.bfloat16)
    ii = np.arange(128)
    same32 = (ii[:, None] // 32) == (ii[None, :] // 32)
    maskB = (ii[:, None] <= ii[None, :]).astype(f32)
    maskC = (ii[:, None] > ii[None, :]).astype(f32)
    masks = np.ascontiguousarray(np.concatenate([maskB, maskC, maskB * same32, maskC * same32], axis=1).astype(f32))
    blockm = same32.astype(f32)
    rowm = (ii[:, None] // 32 == np.arange(4)[None, :]).astype(f32)
    common = dict(
        w_in=w_in_p, w_out=np.ascontiguousarray(np.asarray(w_out, f32)[0]),
        npw=np.ascontiguousarray(np.asarray(norm_pre_w, f32)[0].reshape(8, 128).T),
        w_lr=np.ascontiguousarray(w_in_p[:, CLR:CLR + 16].reshape(8, 128, 16).transpose(1, 0, 2).reshape(128, 128)),
        sinks=np.asarray(attn_sinks, f32)[0],
        w_up=np.asarray(w_gk_up, f32)[0], b_gk=np.asarray(b_gk, f32)[0],
        gnw=np.asarray(gla_norm_w, f32)[0], wpost=np.asarray(norm_post_w, f32)[0],
        ident=ident, masks=masks, blockm=blockm, rowm=rowm,
    )
    in_maps = []
    for c in range(NCORES):
        b, half = c // 2, c % 2
        xm = x_prompt[b, half * 2048:(half + 1) * 2048]
        xp = x_prompt[b, half * 1024:(half + 1) * 1024]
        rot = np.zeros((NMAIN + 2, 128, 32), f32)
        rot[0] = _rot_table(np.arange(1920, 2048))
        for t in range(NMAIN):
            rot[1 + t] = _rot_table(half * 2048 + t * 128 + np.arange(128))
        rot[NMAIN + 1] = _rot_table(4096 + (np.arange(128) % 32))
        t0bias = np.full((128, 1), 0.0 if half == 1 else -30000.0, f32)
        m = dict(common)
        m.update(
            xp=np.ascontiguousarray(xp), xm=np.ascontiguousarray(xm),
            xs=np.ascontiguousarray(x_sample[4 * c:4 * c + 4].reshape(128, D)),
            ck=np.ascontiguousarray(np.asarray(cache_k, f32)[0, 4 * c:4 * c + 4].reshape(4, 128, 128)),
            cv=np.ascontiguousarray(np.asarray(cache_v, f32)[0, 4 * c:4 * c + 4].reshape(4, 128, 128)),
            st=np.ascontiguousarray(np.asarray(state_gla, f32)[0, 4 * c:4 * c + 4]),
            t0bias=t0bias, rot=np.ascontiguousarray(rot.transpose(1, 0, 2).reshape(128, (NMAIN + 2) * 32)), sflag=np.full((128, 1), float(half), f32),
        )
        in_maps.append(m)
    res = run_bass_kernel_spmd(nc, in_maps, core_ids=list(range(NCORES)))
    R = res.results
    y_p = np.zeros((4, 4096, D), f32)
    y_s = np.zeros((32, 32, D), f32)
    nkp = np.zeros((1, 4, 128, 2, 64), f32)
    nvp = np.zeros((1, 4, 128, 2, 64), f32)
    nsp = np.zeros((1, 4, 4, 64, 128), f32)
    nks = np.zeros((1, 32, 32, 2, 64), f32)
    nvs = np.zeros((1, 32, 32, 2, 64), f32)
    nss = np.zeros((1, 32, 4, 64, 128), f32)
    for c in range(NCORES):
        b, half = c // 2, c % 2
        r = R[c]
        y_p[b, half * 2048:(half + 1) * 2048] = r["y_m"]
        y_s[4 * c:4 * c + 4] = r["y_s"].reshape(4, 32, D)
        if half == 1:
            nkp[0, b] = r["kv_p"][:, 0:128].reshape(128, 2, 64)
            nvp[0, b] = r["kv_p"][:, 128:256].reshape(128, 2, 64)
            nsp[0, b] = r["ns_p"]
        nks[0, 4 * c:4 * c + 4] = r["kv_s"][:, 0:128].reshape(4, 32, 2, 64)
        nvs[0, 4 * c:4 * c + 4] = r["kv_s"][:, 128:256].reshape(4, 32, 2, 64)
        nss[0, 4 * c:4 * c + 4] = r["ns_s"]
    return (y_p, y_s, nkp, nvp, nsp, nks, nvs, nss)
```

```python
import math
from contextlib import ExitStack

import numpy as np
import ml_dtypes
import concourse.bass as bass
import concourse.mybir as mybir
from concourse.bass_utils import run_bass_kernel_spmd

F32 = mybir.dt.float32
BF = mybir.dt.bfloat16
AF = mybir.ActivationFunctionType
ALU = mybir.AluOpType

NCORES = 8
D = 1024
DIN = 2832
NPRE = 8
NMAIN = 16
EPS = 1e-6
CQ, CK, CV, CAG, CGQ, CGK, CGV, CGG, CLR = 0, 512, 640, 768, 1280, 1536, 1792, 2304, 2816
WA0, WA1 = 1280, 2832
WB0, WB1 = 0, 1280
LN8 = math.log(0.125)
SKIP_SAMPLE = False
PIPELINE = True
SCHED = 2
STOP_AT = 0
MAXOPS = None
SKIPOPS = ()
JUNKMODE = 0


def th(t):
    return t.tensor if hasattr(t, "tensor") else t


class _Rec:
    def __init__(self):
        self.calls = []

    def __getattr__(self, name):
        if name.startswith("__"):
            raise AttributeError(name)

        def f(*a, **k):
            self.calls.append((name, a, k))
            return self
        return f


def _fsz(ap, skip=1):
    n = 1
    for _, c in list(ap.ap)[skip:]:
        n *= c
    return n


def est_cost(eng, fn):
    if fn is None:
        return 0.0
    rec = _Rec()
    try:
        fn(rec)
        name, a, k = rec.calls[0]
        if name == "matmul":
            rhs = k.get("rhs", a[2] if len(a) > 2 else None)
            n = _fsz(rhs)
            mult = 3.0 if rhs.tensor.dtype == F32 else 1.0
            return max(n, 64) * mult / 2.4 + 10.0
        if name == "transpose":
            return 64.0
        out = k.get("out", a[0] if a else None)
        if name == "dma_start":
            return 2000.0 + _fsz(out, 0) * 4 / 150.0
        n = _fsz(out)
        if eng == "act":
            return 230.0 + 0.85 * n
        if eng == "dve":
            if name in ("tensor_copy", "tensor_scalar"):
                return 90.0 + 0.6 * n
            if name == "reciprocal":
                return 90.0 + 6.5 * n
            return 90.0 + 1.05 * n
        if eng == "pool":
            return 300.0 + 2.2 * n
    except Exception:
        pass
    return 300.0


HOP = 150.0


class Prog:
    def __init__(self):
        self.ops = []
        self.last_w = {}
        self.readers = {}
        self.fin = []
        self.efree = {}

    @staticmethod
    def _norm(k):
        if k.startswith("ps") and "." in k:
            return k.split(".")[0]
        return k

    _cap = None

    def begin(self):
        self._cap = []

    def end(self):
        l = self._cap
        self._cap = None
        return l

    def replay_merged(self, lists):
        units = []
        for li, l in enumerate(lists):
            n = len(l)
            k = 0
            while k < n:
                k2 = k + 1
                if l[k][0] == "pe":
                    while k2 < n and l[k2][0] == "pe":
                        k2 += 1
                units.append(((k + 0.5) / n, li, k, k2))
                k = k2
        units.sort(key=lambda x: (x[0], x[1]))
        maps = [dict() for _ in lists]
        for _, li, k, k2 in units:
            for kk in range(k, k2):
                eng, fn, reads, writes, chan, force, cost = lists[li][kk]
                g = self.add(eng, fn, reads, writes, chan, force=[maps[li][f] for f in force], cost=cost)
                maps[li][kk] = g

    def _deps(self, eng, reads, writes, force):
        deps = set()
        for r in reads:
            if r in self.last_w:
                deps.add(self.last_w[r])
            if r.startswith("ps"):
                for rd in self.readers.get(r, ()):
                    if self.ops[rd]["eng"] != eng:
                        deps.add(rd)
        for w in writes:
            if w in self.last_w:
                deps.add(self.last_w[w])
            for rd in self.readers.get(w, ()):
                deps.add(rd)
        deps.update(force)
        return deps

    def _est_start(self, eng, deps):
        ready = 0.0
        for d in deps:
            ready = max(ready, self.fin[d] + HOP)
        return max(self.efree.get(eng, 0.0), ready)

    def replay_scheduled(self, lists):
        ptr = [0] * len(lists)
        maps = [dict() for _ in lists]
        while True:
            best = None
            for li, l in enumerate(lists):
                if ptr[li] >= len(l):
                    continue
                eng, fn, reads, writes, chan, force, cost = l[ptr[li]]
                reads = [self._norm(k) for k in reads]
                writes = [self._norm(k) for k in writes]
                deps = self._deps(eng, reads, writes, [maps[li][f] for f in force])
                st = self._est_start(eng, deps)
                if best is None or st < best[0] - 1e-6:
                    best = (st, li)
            if best is None:
                break
            li = best[1]
            while True:
                eng, fn, reads, writes, chan, force, cost = lists[li][ptr[li]]
                g = self.add(eng, fn, reads, writes, chan, force=[maps[li][f] for f in force], cost=cost)
                maps[li][ptr[li]] = g
                ptr[li] += 1
                if eng != "pe" or ptr[li] >= len(lists[li]) or lists[li][ptr[li]][0] != "pe":
                    break

    def schedule_all(self, lists, prereqs, prio):
        ptr = {k: 0 for k in lists}
        maps = {k: dict() for k in lists}
        done = set(k for k in lists if len(lists[k]) == 0)
        order = sorted(lists.keys(), key=lambda k: prio[k])
        remaining = [k for k in order if k not in done]
        while remaining:
            best = None
            for k in remaining:
                if not prereqs.get(k, set()) <= done:
                    continue
                eng, fn, reads, writes, chan, force, cost = lists[k][ptr[k]]
                reads = [self._norm(x) for x in reads]
                writes = [self._norm(x) for x in writes]
                deps = self._deps(eng, reads, writes, [maps[k][f] for f in force])
                st = self._est_start(eng, deps)
                if best is None or st < best[0] - 1e-6:
                    best = (st, k)
            assert best is not None, "scheduler deadlock: prerequisites unsatisfiable"
            k = best[1]
            while True:
                eng, fn, reads, writes, chan, force, cost = lists[k][ptr[k]]
                g = self.add(eng, fn, reads, writes, chan, force=[maps[k][f] for f in force], cost=cost)
                maps[k][ptr[k]] = g
                ptr[k] += 1
                if ptr[k] >= len(lists[k]):
                    done.add(k)
                    remaining.remove(k)
                    break
                if eng != "pe" or lists[k][ptr[k]][0] != "pe":
                    break

    def add(self, eng, fn, reads=(), writes=(), chan=None, force=(), cost=None):
        if cost is None:
            cost = est_cost(eng, fn)
        if self._cap is not None:
            self._cap.append((eng, fn, list(reads), list(writes), chan, list(force), cost))
            return len(self._cap) - 1
        idx = len(self.ops)
        reads = [self._norm(k) for k in reads]
        writes = [self._norm(k) for k in writes]
        deps = self._deps(eng, reads, writes, force)
        deps.discard(idx)
        st = self._est_start(eng, deps)
        if chan is not None:
            self.efree[eng] = st + 60.0
            self.fin.append(st + cost)
        else:
            self.efree[eng] = st + cost
            self.fin.append(st + cost)
        self.ops.append(dict(eng=eng, fn=fn, deps=sorted(deps), chan=chan, force=set(force)))
        for r in reads:
            self.readers.setdefault(r, []).append(idx)
        for w in writes:
            self.last_w[w] = idx
            self.readers[w] = []
        return idx

    def emit(self, nc, stack):
        if MAXOPS is not None and MAXOPS < len(self.ops):
            self.ops = self.ops[:MAXOPS]
            dmas = [i for i, o in enumerate(self.ops) if o["chan"] is not None]
            self.ops.append(dict(eng="sp", fn=None, deps=dmas, chan=None, force=set()))
        ops = self.ops
        for i in SKIPOPS:
            if i < len(ops):
                ops[i]["fn"] = None if ops[i]["chan"] is None else ops[i]["fn"]
        needs = [False] * len(ops)
        for i, o in enumerate(ops):
            for d in o["deps"]:
                p = ops[d]
                if p["chan"] is not None:
                    needs[d] = True
                elif d in o.get("force", ()):
                    needs[d] = True
                elif p["eng"] == o["eng"] and o["chan"] is None and p["eng"] in ("pe", "sp"):
                    continue
                else:
                    needs[d] = True
        sems = {}

        def sem(name):
            if name not in sems:
                sems[name] = stack.enter_context(nc.semaphore(name))
            return sems[name]

        cnt = {}
        val = [None] * len(ops)
        for i, o in enumerate(ops):
            if o["chan"] is not None:
                nm = "c_" + o["chan"]
                cnt[nm] = cnt.get(nm, 0) + (1 if o["chan"].startswith("cc") else 16)
                val[i] = (nm, cnt[nm])
            elif needs[i]:
                nm = "e_" + o["eng"]
                cnt[nm] = cnt.get(nm, 0) + 1
                val[i] = (nm, cnt[nm])
        for nm in cnt:
            sem(nm)
        per_eng = {}
        for i, o in enumerate(ops):
            per_eng.setdefault(o["eng"], []).append(i)
        block = stack.enter_context(nc.Block())

        def make(engname):
            def body(e):
                seen = {}
                for i in per_eng.get(engname, []):
                    o = ops[i]
                    for d in o["deps"]:
                        p = ops[d]
                        if val[d] is None:
                            continue
                        if (p["chan"] is None and p["eng"] == engname and o["chan"] is None
                                and engname in ("pe", "sp") and d not in o.get("force", ())):
                            continue
                        nm, v = val[d]
                        if seen.get(nm, 0) >= v:
                            continue
                        seen[nm] = v
                        e.wait_ge(sems[nm], v)
                    if o["fn"] is None:
                        continue
                    ins = o["fn"](e)
                    if val[i] is not None:
                        nm, v = val[i]
                        ins.then_inc(sems[nm], 16 if (o["chan"] is not None and not o["chan"].startswith("cc")) else 1)
            return body

        block.sync(make("sp"))
        block.tensor(make("pe"))
        block.scalar(make("act"))
        block.vector(make("dve"))
        block.gpsimd(make("pool"))


def build_program():
    nc = bass.Bass("TRN2", target_bir_lowering=False)
    stack = ExitStack()
    P = Prog()

    def din(name, shape, dt=F32):
        return nc.dram_tensor(name, list(shape), dt, kind="ExternalInput").ap()

    def dout(name, shape, dt=F32):
        return nc.dram_tensor(name, list(shape), dt, kind="ExternalOutput").ap()

    xp_d = din("xp", [NPRE * 128, D])
    xm_d = din("xm", [NMAIN * 128, D])
    xs_d = din("xs", [128, D])
    ck_d = din("ck", [4, 128, 128])
    cv_d = din("cv", [4, 128, 128])
    st_d = din("st", [4, 4, 64, 128])
    win_d = din("w_in", [D, DIN])
    wout_d = din("w_out", [D, D])
    npw_d = din("npw", [128, 8])
    wlr_d = din("w_lr", [128, 128])
    sink_d = din("sinks", [8])
    wup_d = din("w_up", [16, 256])
    bgk_d = din("b_gk", [256])
    gnw_d = din("gnw", [128])
    wpost_d = din("wpost", [D])
    ident_d = din("ident", [128, 128], BF)
    masks_d = din("masks", [128, 512])
    blockm_d = din("blockm", [128, 128])
    rowm_d = din("rowm", [128, 4])
    t0b_d = din("t0bias", [128, 1])
    sflag_d = din("sflag", [128, 1])
    cc_in = nc.dram_tensor("cc_in", [128, 256], F32, kind="Internal")
    cc_out = nc.dram_tensor("cc_out", [256, 256], F32, kind="Internal")
    rot_d = din("rot", [128, (NMAIN + 2) * 32])

    ym_d = dout("y_m", [NMAIN * 128, D])
    ys_d = dout("y_s", [128, D])
    kvp_d = dout("kv_p", [128, 256])
    kvs_d = dout("kv_s", [128, 256])
    nsp_d = dout("ns_p", [4, 64, 128])
    nss_d = dout("ns_s", [4, 4, 64, 128])

    def T(name, free, dt=F32, parts=128):
        return stack.enter_context(nc.sbuf_tensor(name, [parts, free], dt))

    def PS(name, free, dt=F32):
        return stack.enter_context(nc.psum_tensor(name, [128, free], dt))

    def V(t, p0, npart, off, dims):
        Fr = 1
        for s in t.shape[1:]:
            Fr *= s
        return bass.AP(th(t), p0 * Fr + off, [[Fr, npart]] + [list(d) for d in dims])

    w_in_bf = T("w_in_bf", 8 * DIN, BF)
    w_out_bf = T("w_out_bf", 8 * D, BF)
    NWS = 4
    wstage = [T(f"wstage{i}", 776) for i in range(NWS)]
    wlr_f = T("wlr_f", 128)
    xs = [T(f"xs{i}", D) for i in range(5)]
    xh = T("xh", D, BF)
    hT = [T(f"hT{i}", D, BF) for i in range(3)]
    junk = T("junk", D, BF)
    stat = T("stat", 8)
    tm = T("tm", 1152, BF)
    fT = [T(f"fT{i}", 9 * 128, BF) for i in range(2)]
    kT = [T(f"kT{i}", 128, BF) for i in range(3)]
    vaug = [T(f"vaug{i}", 130, BF) for i in range(3)]
    pT = T("pT", 2 * 2 * 512, BF)
    gate = [T(f"gate{i}", D, BF) for i in range(2)]
    sgt = T("sgt", 512)
    sgt2 = T("sgt2", 512)
    e1 = T("e1", 256)
    lsp = T("lsp", 256)
    Eb = [T(f"Eb{i}", 256) for i in range(2)]
    Einv = [T(f"Einv{i}", 256) for i in range(2)]
    ec = [T(f"ec{i}", 256) for i in range(2)]
    dec = [T(f"dec{i}", 8) for i in range(3)]
    kp_bf = [T(f"kp_bf{i}", 256, BF) for i in range(2)]
    v_bf = [T(f"v_bf{i}", 512, BF) for i in range(2)]
    AT_bf = T("AT_bf", 512, BF)
    S = T("S", 256)
    Srecv = T("Srecv", 256)
    Dtot = T("Dtot", 2)
    sflag = T("sflag_sb", 1)
    S_bf = T("S_bf", 256, BF)
    gst = T("gst", 16)
    den = T("den", 16)
    mix_tm = T("mix_tm", D, BF)
    mixT = T("mixT", D, BF)
    pst = T("pst", 8)
    tt = T("tt", D)
    ysb = [T(f"ysb{i}", D) for i in range(2)]
    kvout = T("kvout", 256)
    rt1 = T("rt1", 160)
    rt2 = T("rt2", 160)
    glr_tm = T("glr_tm", 16, BF)
    glrT = T("glrT", 128, BF)
    wup_f = T("wup_f", 256)
    wup_bf = T("wup_bf", 256, BF)
    npw_sb = T("npw_sb", 8)
    gnw_sb = T("gnw_sb", 1)
    sk_sb = T("sk_sb", 8)
    es_bc = T("es_bc", 8)
    wpost_bc = T("wpost_bc", D)
    ident = T("ident_sb", 128, BF)
    masks_f = T("masks_f", 4 * 128)
    maskB_bf = T("maskB_bf", 128, BF)
    maskB32_bf = T("maskB32_bf", 128, BF)
    blockm_f = T("blockm_f", 128)
    blockm_bf = T("blockm_bf", 128, BF)
    rowm = T("rowm_sb", 4)
    t0b = T("t0b_sb", 1)
    ones_f = T("ones_f", 1)
    rot = T("rot_sb", (NMAIN + 2) * 32)
    ck_f = T("ck_f", 512)
    cv_f = T("cv_f", 512)
    ck_b = T("ck_b", 512, BF)
    kcT = T("kcT", 512, BF)
    vaug_c = T("vaug_c", 4 * 130, BF)
    Pprev = T("Pprev", 8 * 4 * 128, BF)
    pTo = T("pTo", 1024, BF)
    S0 = T("S0", 4 * 256)
    S0_bf = T("S0_bf", 4 * 256, BF)
    Qs = T("Qs", 2 * 4 * 128, BF)
    kpm = T("kpm", 4 * 256, BF)
    nst = T("nst", 4 * 256)

    psT = PS("psT", 1024, BF)
    psZ = [PS(f"psZ{i}", 512) for i in range(2)]
    psZb = [psZ[i].bitcast(BF) for i in range(2)]
    psG = PS("psG", 512)
    psE = PS("psE", 512)
    psE_bf = psE.bitcast(BF)
    psBk = [PS(f"psB{i}", 512) for i in range(3)]
    psB0b = psBk[0].bitcast(BF)

    def dma(out_ap, in_ap, reads, writes, chan, slow=False):
        if slow:
            P.add("sp", lambda e, o=out_ap, i=in_ap: e.dma_start(out=o, in_=i, allow_slow_non_contiguous=True),
                  reads=reads, writes=writes, chan=chan)
        else:
            P.add("sp", lambda e, o=out_ap, i=in_ap: e.dma_start(out=o, in_=i), reads=reads, writes=writes, chan=chan)

    dma(xs[0][:], xp_d[0:128, :], [], ["xs0"], "xs0")
    dma(ident[:], ident_d, [], ["ident"], "ident")
    dma(masks_f[:], masks_d, [], ["masks_f"], "masks")
    dma(blockm_f[:], blockm_d, [], ["blockm_f"], "blockm")
    dma(rowm[:], rowm_d, [], ["rowm"], "rowm")
    dma(t0b[:], t0b_d, [], ["t0b"], "t0b")
    dma(sflag[:], sflag_d, [], ["sflag"], "sflag")
    dma(rot[:], rot_d, [], ["rot"], "rot")
    dma(npw_sb[:], npw_d, [], ["npw"], "npw")
    dma(gnw_sb[:], bass.AP(gnw_d.tensor, 0, [[1, 128], [1, 1]]), [], ["gnw"], "gnw")
    dma(sk_sb[:], bass.AP(sink_d.tensor, 0, [[0, 128], [1, 8]]), [], ["sk"], "sk")
    dma(wpost_bc[:], bass.AP(wpost_d.tensor, 0, [[0, 128], [1, D]]), [], ["wpost"], "wpost")
    dma(wup_f[0:16, :], wup_d, [], ["wup_f.a"], "wupa")
    dma(wup_f[16:17, :], bass.AP(bgk_d.tensor, 0, [[256, 1], [1, 256]]), [], ["wup_f.b"], "wupb")

    P.add("dve", lambda e: e.tensor_copy(wup_bf[0:17, :], wup_f[0:17, :]), ["wup_f.a", "wup_f.b"], ["wup_bf"])
    P.add("dve", lambda e: e.tensor_copy(maskB_bf[:], masks_f[:, 0:128]), ["masks_f"], ["maskB_bf"])
    P.add("dve", lambda e: e.tensor_copy(maskB32_bf[:], masks_f[:, 256:384]), ["masks_f"], ["maskB32_bf"])
    P.add("dve", lambda e: e.tensor_copy(blockm_bf[:], blockm_f[:]), ["blockm_f"], ["blockm_bf"])
    P.add("pool", lambda e: e.memset(glrT[0:32, :], 1.0), [], ["glrT"])
    P.add("pool", lambda e: e.memset(vaug[0][:], 1.0), [], ["vaug0"])
    P.add("pool", lambda e: e.memset(vaug[1][:], 1.0), [], ["vaug1"])
    P.add("pool", lambda e: e.memset(vaug[2][:], 1.0), [], ["vaug2"])
    P.add("pool", lambda e: e.memset(vaug_c[:], 1.0), [], ["vaug_c"])
    P.add("pool", lambda e: e.memset(ones_f[:], 1.0), [], ["ones_f"])
    P.add("pool", lambda e: e.memset(S[:], 0.0), [], ["S0k", "S1k"])
    P.add("pool", lambda e: e.memset(Dtot[:], 1.0), [], ["Dtot"])
    P.add("pool", lambda e: e.memset(S_bf[:], 0.0), [], ["S_bf"])
    P.add("pool", lambda e: e.memset(Pprev[:], 0.0), [], ["Pprev"])
    P.add("pool", lambda e: e.memset(pT[:], 0.0), [], ["pT00", "pT01", "pT10", "pT11"])
    P.add("pool", lambda e: e.memset(Qs[:], 0.0), [], ["Qs"])
    P.add("act", lambda e: e.activation(es_bc[:], sk_sb[:], AF.Exp), ["sk"], ["es_bc"])

    wcount = [0]

    def cast_scaled(eng, out_ap, in_ap, scal_ap, reads, writes):
        if eng == "act":
            if scal_ap is None:
                P.add("act", lambda e: e.copy(out_ap, in_ap), reads, writes)
            else:
                P.add("act", lambda e: e.activation(out_ap, in_ap, AF.Copy, scale=scal_ap), reads, writes)
        else:
            if scal_ap is None:
                P.add("dve", lambda e: e.tensor_copy(out_ap, in_ap), reads, writes)
            else:
                P.add("dve", lambda e: e.tensor_scalar(out_ap, in_ap, scal_ap, None, ALU.mult), reads, writes)

    def load_w_in(k, c0, c1):
        i = wcount[0] % NWS
        wcount[0] += 1
        n = c1 - c0
        assert n <= 776
        dma(wstage[i][:, 0:n], win_d[k * 128:(k + 1) * 128, c0:c1], [], [f"wstage{i}"], f"wstage{i}")
        eng = "dve" if (wcount[0] % 2 == 0) else "act"
        cast_scaled(eng, w_in_bf[:, k * DIN + c0:k * DIN + c0 + n], wstage[i][:, 0:n], npw_sb[:, k:k + 1],
                    [f"wstage{i}", "npw"], [f"w_in.{k}.{c0}"])

    def load_w_out(k, half):
        i = wcount[0] % NWS
        wcount[0] += 1
        c0 = half * 512
        dma(wstage[i][:, 0:512], wout_d[k * 128:(k + 1) * 128, c0:c0 + 512], [], [f"wstage{i}"], f"wstage{i}")
        eng = "dve" if (wcount[0] % 2 == 0) else "act"
        cast_scaled(eng, w_out_bf[:, k * D + c0:k * D + c0 + 512], wstage[i][:, 0:512], gnw_sb[:, 0:1] if k >= 4 else None,
                    [f"wstage{i}", "gnw"], [f"w_out.{k}.{half}"])

    def wkeys(c0):
        return [f"w_in.{k}.{c0}" for k in range(8)]

    W_GKGV = wkeys(CGK)
    W_GQ = wkeys(CGQ)
    W_GG = wkeys(CGG)
    W_LR = ["w_in.lr"]
    WINB = wkeys(0) + wkeys(640)

    def load_w_lr():
        dma(wlr_f[:], wlr_d, [], ["wlr_f"], "wlr")
        P.add("dve", lambda e: e.tensor_tensor(V(w_in_bf, 0, 128, CLR, [[DIN, 8], [1, 16]]), V(wlr_f, 0, 128, 0, [[16, 8], [1, 16]]),
                                               V(npw_sb, 0, 128, 0, [[1, 8], [0, 16]]), ALU.mult), ["wlr_f", "npw"], ["w_in.lr"])
    WOUT = [f"w_out.{k}.{half}" for k in range(8) for half in range(2)]

    NT = NPRE + NMAIN + 1

    def x_src(i):
        if i < NPRE:
            return xp_d[i * 128:(i + 1) * 128, :]
        if i < NPRE + NMAIN:
            j = i - NPRE
            return xm_d[j * 128:(j + 1) * 128, :]
        return xs_d

    def load_x(i):
        slot = i % 5
        dma(xs[slot][:], x_src(i), [], [f"xs{slot}"], f"xs{slot}")

    def front(i):
        slot, p = i % 5, i % 3
        X = f"xs{slot}"
        hTp = hT[p]
        P.add("act", lambda e: e.activation(junk[:], xs[slot][:], AF.Square, accum_out=stat[:, 0:1]), [X], ["stat0"])
        P.add("act", lambda e: e.activation(stat[:, 1:2], stat[:, 0:1], AF.Ln, scale=1.0 / D, bias=EPS), ["stat0"], ["stat1"])
        P.add("act", lambda e: e.activation(stat[:, 2:3], stat[:, 1:2], AF.Exp, scale=-0.5), ["stat1"], ["stat2"])
        P.add("dve", lambda e: e.tensor_scalar(xh[:], xs[slot][:], stat[:, 2:3], None, ALU.mult), [X, "stat2"], ["xh"])
        for k in range(8):
            P.add("pe", lambda e, k=k: e.transpose(psT[:, k * 128:(k + 1) * 128], xh[:, k * 128:(k + 1) * 128], ident[:]),
                  ["xh", "ident"], ["psT"])
        P.add("act", lambda e: e.copy(hTp[:, 0:512], psT[:, 0:512]), ["psT"], [f"hT{p}.a"])
        P.add("dve", lambda e: e.tensor_copy(hTp[:, 512:1024], psT[:, 512:1024]), ["psT"], [f"hT{p}.b"])

    def inproj(i, c0, c1, zi, wkeys):
        n = c1 - c0
        p = i % 3
        hTp = hT[p]
        for k in range(8):
            P.add("pe", lambda e, k=k: e.matmul(psZ[zi][:, 0:n], lhsT=hTp[:, k * 128:(k + 1) * 128],
                                                 rhs=w_in_bf[:, k * DIN + c0:k * DIN + c1],
                                                 start=(k == 0), stop=(k == 7)),
                  [f"hT{p}.a", f"hT{p}.b"] + wkeys, [f"psZ{zi}"])

    def gates_front(i, mB, mC, sample):
        p, p3 = i % 2, i % 3
        hTp, Ebp, Einvp, ecp, decp = hT[p3], Eb[p], Einv[p], ec[p], dec[p3]
        for k in range(8):
            P.add("pe", lambda e, k=k: e.matmul(psE[:, 0:16], lhsT=hTp[:, k * 128:(k + 1) * 128],
                                                 rhs=w_in_bf[:, k * DIN + CLR:k * DIN + CLR + 16], start=(k == 0), stop=(k == 7)),
                  [f"hT{p3}.a", f"hT{p3}.b"] + W_LR, ["psE"])
        P.add("act", lambda e: e.copy(glr_tm[:, 0:16], psE[:, 0:16]), ["psE"], ["glr_tm"])
        P.add("pe", lambda e: e.transpose(psE_bf[0:16, 384:512], glr_tm[:, 0:16], ident[:]), ["glr_tm", "ident"], ["psE"])
        P.add("act", lambda e: e.copy(glrT[0:16, :], psE_bf[0:16, 384:512]), ["psE"], ["glrT"])
        P.add("pe", lambda e: e.matmul(psE[:, 256:512], lhsT=glrT[0:17, :], rhs=wup_bf[0:17, :], start=True, stop=True),
              ["glrT", "wup_bf"], ["psE"])
        P.add("act", lambda e: e.activation(e1[:], psE[:, 256:512], AF.Exp, scale=-1.0), ["psE"], ["e1"])
        P.add("act", lambda e: e.activation(lsp[:], e1[:], AF.Ln, bias=1.0), ["e1"], ["lsp"])
        if mB is not None:
            P.add("pe", lambda e: e.matmul(psG[:, 0:256], lhsT=masks_f[:, mB * 128:(mB + 1) * 128], rhs=lsp[:],
                                           start=True, stop=True), ["lsp", "masks_f"], ["psG"])
        P.add("pe", lambda e: e.matmul(psG[:, 256:512], lhsT=masks_f[:, mC * 128:(mC + 1) * 128], rhs=lsp[:],
                                       start=True, stop=True), ["lsp", "masks_f"], ["psG"])
        if not sample:
            for j in range(2):
                P.add("pe", lambda e, j=j: e.matmul(psE[:, 128 + j:129 + j], lhsT=lsp[:, j * 128:(j + 1) * 128],
                                                     rhs=ones_f[:, 0:1], start=True, stop=True), ["lsp", "ones_f"], ["psE"])
            P.add("act", lambda e: e.activation(decp[:, 0:2], psE[:, 128:130], AF.Exp, scale=-1.0 / 16), ["psE"], [f"dec{p3}"])
        else:
            for j in range(2):
                P.add("pe", lambda e, j=j: e.matmul(psE[:, 128 + 4 * j:132 + 4 * j], lhsT=lsp[:, j * 128:(j + 1) * 128],
                                                     rhs=rowm[:, 0:4], start=True, stop=True), ["lsp", "rowm"], ["psE"])
            P.add("act", lambda e: e.activation(decp[:, 0:8], psE[:, 128:136], AF.Exp, scale=-1.0 / 16), ["psE"], [f"dec{p3}"])
        if mB is not None:
            P.add("act", lambda e: e.activation(Ebp[:], psG[:, 0:256], AF.Exp, scale=-1.0 / 16, bias=LN8), ["psG"], [f"Eb{p}"])
            P.add("act", lambda e: e.activation(Einvp[:], psG[:, 0:256], AF.Exp, scale=1.0 / 16), ["psG"], [f"Einv{p}"])
        P.add("act", lambda e: e.activation(ecp[:], psG[:, 256:512], AF.Exp, scale=-1.0 / 16), ["psG"], [f"ec{p}"])

    def silu_gate(i, zi, g0):
        p = i % 2
        gp = gate[p]
        Z = f"psZ{zi}"
        P.add("act", lambda e: e.activation(sgt[:], psZ[zi][:], AF.Exp, scale=-1.0), [Z], ["sgt"])
        P.add("act", lambda e: e.activation(sgt2[:], sgt[:], AF.Ln, bias=1.0), ["sgt"], ["sgt2"])
        P.add("act", lambda e: e.activation(sgt[:], sgt2[:], AF.Exp, scale=-1.0), ["sgt2"], ["sgt"])
        P.add("dve", lambda e: e.tensor_tensor(gp[:, g0:g0 + 512], psZ[zi][:], sgt[:], ALU.mult), [Z, "sgt"], [f"gate{p}.{g0}"])

    def rotary(zi, col0, nh, rti, dst_t, dst_off, dst_key):
        Z = f"psZ{zi}"
        src16 = V(psZ[zi], 0, 128, col0, [[64, nh], [1, 16]])
        srcsw = V(psZ[zi], 0, 128, col0 + 8, [[64, nh], [-8, 2], [1, 8]])
        cbc = V(rot, 0, 128, rti * 32, [[0, nh], [1, 16]])
        sbc = V(rot, 0, 128, rti * 32 + 16, [[0, nh], [8, 2], [1, 8]])
        t1 = V(rt1, 0, 128, 0, [[16, nh], [1, 16]])
        t2 = V(rt2, 0, 128, 0, [[16, nh], [8, 2], [1, 8]])
        t2f = V(rt2, 0, 128, 0, [[16, nh], [1, 16]])
        d16 = V(dst_t, 0, 128, dst_off, [[64, nh], [1, 16]])
        d48 = V(dst_t, 0, 128, dst_off + 16, [[64, nh], [1, 48]])
        s48 = V(psZ[zi], 0, 128, col0 + 16, [[64, nh], [1, 48]])
        P.add("dve", lambda e: e.tensor_tensor(t1, src16, cbc, ALU.mult), [Z, "rot"], ["rt1"])
        P.add("dve", lambda e: e.tensor_tensor(t2, srcsw, sbc, ALU.mult), [Z, "rot"], ["rt2"])
        P.add("dve", lambda e: e.tensor_tensor(d16, t1, t2f, ALU.add), ["rt1", "rt2"], [dst_key + ".r"])
        P.add("dve", lambda e: e.tensor_copy(d48, s48), [Z], [dst_key + ".p"])

    def kv_out(zi, dst_d, okey):
        P.add("dve", lambda e: e.tensor_tensor(V(kvout, 0, 128, 0, [[64, 2], [1, 16]]), V(rt1, 0, 128, 0, [[16, 2], [1, 16]]),
                                               V(rt2, 0, 128, 0, [[16, 2], [1, 16]]), ALU.add), ["rt1", "rt2"], ["kvout.a"])
        P.add("dve", lambda e: e.tensor_copy(V(kvout, 0, 128, 16, [[64, 2], [1, 48]]), V(psZ[zi], 0, 128, 16, [[64, 2], [1, 48]])),
              [f"psZ{zi}"], ["kvout.b"])
        P.add("dve", lambda e: e.tensor_copy(kvout[:, 128:256], psZ[zi][:, 128:256]), [f"psZ{zi}"], ["kvout.c"])
        dma(dst_d, kvout[:], ["kvout.a", "kvout.b", "kvout.c"], [okey], "kvout")
        outkeys.append(okey)

    def state_update(i, acc_decay=False):
        p, p3 = i % 2, i % 3
        kpp, vp, decp = kp_bf[p], v_bf[p], dec[p3]
        if acc_decay:
            P.add("dve", lambda e: e.tensor_tensor(Dtot[:, 0:2], Dtot[:, 0:2], decp[:, 0:2], ALU.mult),
                  ["Dtot", f"dec{p3}"], ["Dtot"])
        for h in range(4):
            j, b0 = h // 2, (h % 2) * 64
            P.add("pe", lambda e, h=h, j=j, b0=b0: e.matmul(psBk[2][b0:b0 + 64, j * 128:(j + 1) * 128],
                                                             lhsT=kpp[:, h * 64:(h + 1) * 64],
                                                             rhs=vp[:, h * 128:(h + 1) * 128], start=True, stop=True),
                  [f"kp_bf{p}", f"v_bf{p}"], ["psB2"])
        for j in range(2):
            P.add("dve", lambda e, j=j: e.scalar_tensor_tensor(S[:, j * 128:(j + 1) * 128], S[:, j * 128:(j + 1) * 128],
                                                               decp[:, j:j + 1], psBk[2][:, j * 128:(j + 1) * 128],
                                                               ALU.mult, ALU.add),
                  [f"S{j}k", f"dec{p3}", "psB2"], [f"S{j}k"])
        P.add("dve", lambda e: e.tensor_copy(S_bf[:], S[:]), ["S0k", "S1k"], ["S_bf"])

    def attn_finish(i, g):
        p = i % 2
        gp = gate[p]
        P.add("dve", lambda e: e.tensor_tensor(den[:, g * 4:(g + 1) * 4], V(psBk[2], 0, 128, 64, [[65, 4]]),
                                               es_bc[:, g * 4:(g + 1) * 4], ALU.add), ["psB2", "es_bc"], [f"den{g}"])
        P.add("dve", lambda e: e.reciprocal(den[:, 8 + g * 4:12 + g * 4], den[:, g * 4:(g + 1) * 4]), [f"den{g}"], [f"rec{g}"])
        for b in range(4):
            h = g * 4 + b
            P.add("dve", lambda e, h=h, b=b: e.scalar_tensor_tensor(
                mix_tm[:, h * 64:(h + 1) * 64], psBk[2][:, b * 65:b * 65 + 64], den[:, 8 + h:9 + h],
                gp[:, h * 64:(h + 1) * 64], ALU.mult, ALU.mult),
                ["psB2", f"rec{g}", f"gate{p}.0"], [f"mix.a{h}"])

    def gla_finish(i):
        p = i % 2
        gp = gate[p]
        for h in range(4):
            P.add("act", lambda e, h=h: e.activation(junk[:, h * 128:(h + 1) * 128], psBk[1][:, h * 128:(h + 1) * 128], AF.Square,
                                                     accum_out=gst[:, h:h + 1]), ["psB1"], [f"gst{h}"])
        P.add("act", lambda e: e.activation(gst[:, 4:8], gst[:, 0:4], AF.Ln, scale=1.0 / 128, bias=EPS),
              [f"gst{h}" for h in range(4)], ["gstl"])
        P.add("act", lambda e: e.activation(gst[:, 8:12], gst[:, 4:8], AF.Exp, scale=-0.5), ["gstl"], ["grs"])
        for h in range(4):
            P.add("dve", lambda e, h=h: e.scalar_tensor_tensor(
                mix_tm[:, 512 + h * 128:512 + (h + 1) * 128], psBk[1][:, h * 128:(h + 1) * 128], gst[:, 8 + h:9 + h],
                gp[:, 512 + h * 128:512 + (h + 1) * 128], ALU.mult, ALU.mult),
                ["psB1", "grs", f"gate{p}.512"], [f"mix.g{h}"])

    ycount = [0]

    def out_proj(i, dst_rows, okey):
        xslot = i % 5
        mixkeys = [f"mix.a{h}" for h in range(8)] + [f"mix.g{h}" for h in range(4)]
        for k in range(8):
            P.add("pe", lambda e, k=k: e.transpose(psB0b[:, k * 128:(k + 1) * 128], mix_tm[:, k * 128:(k + 1) * 128], ident[:]),
                  mixkeys + ["ident"], ["psB0"])
        P.add("act", lambda e: e.copy(mixT[:, 0:512], psB0b[:, 0:512]), ["psB0"], ["mixT.a"])
        P.add("dve", lambda e: e.tensor_copy(mixT[:, 512:1024], psB0b[:, 512:1024]), ["psB0"], ["mixT.b"])
        for half in range(2):
            for k in range(8):
                P.add("pe", lambda e, k=k, half=half: e.matmul(psBk[1 + half][:], lhsT=mixT[:, k * 128:(k + 1) * 128],
                                                                rhs=w_out_bf[:, k * D + half * 512:k * D + (half + 1) * 512],
                                                                start=(k == 0), stop=(k == 7)),
                      ["mixT.a", "mixT.b"] + WOUT, [f"psB{1 + half}"])
        for half in range(2):
            P.add("act", lambda e, half=half: e.activation(junk[:, half * 512:(half + 1) * 512], psBk[1 + half][:], AF.Square,
                                                           accum_out=pst[:, half:half + 1]), [f"psB{1 + half}"], [f"pst{half}"])
        P.add("dve", lambda e: e.tensor_tensor(pst[:, 2:3], pst[:, 0:1], pst[:, 1:2], ALU.add), ["pst0", "pst1"], ["pst2"])
        P.add("act", lambda e: e.activation(pst[:, 3:4], pst[:, 2:3], AF.Ln, scale=1.0 / D, bias=EPS), ["pst2"], ["pst3"])
        P.add("act", lambda e: e.activation(pst[:, 4:5], pst[:, 3:4], AF.Exp, scale=-0.5), ["pst3"], ["prs"])
        for half in range(2):
            P.add("dve", lambda e, half=half: e.scalar_tensor_tensor(
                tt[:, half * 512:(half + 1) * 512], psBk[1 + half][:], pst[:, 4:5], wpost_bc[:, half * 512:(half + 1) * 512],
                ALU.mult, ALU.mult), [f"psB{1 + half}", "prs", "wpost"], [f"tt{half}"])
        ys_i = ycount[0] % 2
        ycount[0] += 1
        P.add("pool", lambda e: e.tensor_tensor(ysb[ys_i][:], tt[:], xs[xslot][:], ALU.add),
              ["tt0", "tt1", f"xs{xslot}"], [f"ysb{ys_i}"])
        dma(dst_rows, ysb[ys_i][:], [f"ysb{ys_i}"], [okey], f"ysb{ys_i}")
        outkeys.append(okey)

    outkeys = []
    TMK = {0: ["tm.q.r", "tm.q.p"], 1: ["tm.q.r", "tm.q.p"], 2: ["tm.q.r", "tm.q.p"], 3: ["tm.q.r", "tm.q.p"],
           4: ["tm.k.r", "tm.k.p"], 5: ["tm.qin"], 6: ["tm.qin"], 7: ["tm.kout"], 8: ["tm.kout"]}

    def fT_transposes(i):
        p, p3 = i % 2, i % 3
        fTp, kTp = fT[p], kT[p3]
        for b in range(9):
            dst = psZb[0][:, b * 128:(b + 1) * 128] if b < 8 else psZb[1][:, 0:128]
            P.add("pe", lambda e, b=b, dst=dst: e.transpose(dst, tm[:, b * 128:(b + 1) * 128], ident[:]),
                  TMK[b] + ["ident"], ["psZ0" if b < 8 else "psZ1"])
        P.add("dve", lambda e: e.tensor_copy(fTp[:, 0:512], psZb[0][:, 0:512]), ["psZ0"], [f"fT{p}.q"])
        P.add("act", lambda e: e.copy(kTp[:], psZb[0][:, 512:640]), ["psZ0"], [f"kT{p3}"])
        P.add("act", lambda e: e.copy(fTp[:, 640:1024], psZb[0][:, 640:1024]), ["psZ0"], [f"fT{p}.g"])
        P.add("act", lambda e: e.copy(fTp[:, 1024:1152], psZb[1][:, 0:128]), ["psZ1"], [f"fT{p}.8"])

    def kind(i):
        return "pre" if i < NPRE else ("main" if i < NPRE + NMAIN else "smp")

    def stage_A0(i):
        if i + 1 < NT:
            load_x(i + 1)
        front(i)

    def stage_A1(i):
        k = kind(i)
        if k == "pre":
            gates_front(i, None, 1, False)
        elif k == "main":
            gates_front(i, 0, 1, False)
        else:
            gates_front(i, 2, 3, True)

    def stage_A2(i):
        k = kind(i)
        p, p3 = i % 2, i % 3
        Ebp, Einvp, ecp, kpp, vp, gp = Eb[p], Einv[p], ec[p], kp_bf[p], v_bf[p], gate[p]
        vaugp = vaug[p3]
        if k == "pre":
            inproj(i, CGK, CGK + 256, 0, W_GKGV)
            P.add("dve", lambda e: e.tensor_tensor(kpp[:], psZ[0][:, 0:256], ecp[:], ALU.mult), ["psZ0", f"ec{p}"], [f"kp_bf{p}"])
            inproj(i, CGV, CGV + 512, 1, W_GKGV)
            P.add("act", lambda e: e.copy(vp[:], psZ[1][:]), ["psZ1"], [f"v_bf{p}"])
            if i == NPRE - 1:
                inproj(i, CK, CK + 256, 0, WINB)
                rotary(0, 0, 2, 0, tm, 512, "tm.k")
                P.add("dve", lambda e: e.tensor_copy(V(vaugp, 0, 128, 0, [[65, 2], [1, 64]]),
                                                     V(psZ[0], 0, 128, 128, [[64, 2], [1, 64]])), ["psZ0"], [f"vaug{p3}"])
                kTp = kT[p3]
                P.add("pe", lambda e: e.transpose(psZb[1][:, 0:128], tm[:, 512:640], ident[:]), ["tm.k.r", "tm.k.p", "ident"], ["psZ1"])
                P.add("dve", lambda e: e.tensor_copy(kTp[:], psZb[1][:, 0:128]), ["psZ1"], [f"kT{p3}"])
            return
        t = i - NPRE
        rti = 1 + t if k == "main" else NMAIN + 1
        inproj(i, CGQ, CGQ + 512, 0, W_GQ + W_GKGV)
        P.add("dve", lambda e: e.tensor_tensor(tm[:, 640:896], psZ[0][:, 0:256], Ebp[:], ALU.mult), ["psZ0", f"Eb{p}"], ["tm.qin"])
        P.add("dve", lambda e: e.tensor_tensor(tm[:, 896:1152], psZ[0][:, 256:512], Einvp[:], ALU.mult), ["psZ0", f"Einv{p}"], ["tm.kout"])
        P.add("dve", lambda e: e.tensor_tensor(kpp[:], psZ[0][:, 256:512], ecp[:], ALU.mult), ["psZ0", f"ec{p}"], [f"kp_bf{p}"])
        if k == "smp":
            for s_ in range(4):
                P.add("pool", lambda e, s_=s_: e.tensor_scalar(kpm[:, s_ * 256:(s_ + 1) * 256], kpp[:], rowm[:, s_:s_ + 1], None, ALU.mult),
                      [f"kp_bf{p}", "rowm"], [f"kpm{s_}"])
        inproj(i, CGV, CGV + 512, 1, W_GKGV)
        P.add("act", lambda e: e.copy(vp[:], psZ[1][:]), ["psZ1"], [f"v_bf{p}"])
        inproj(i, CGG, CGG + 512, 0, W_GG)
        silu_gate(i, 0, 512)
        inproj(i, CAG, CAG + 512, 1, WINB)
        silu_gate(i, 1, 0)
        inproj(i, CQ, CQ + 512, 0, WINB)
        rotary(0, 0, 8, rti, tm, 0, "tm.q")
        inproj(i, CK, CK + 256, 1, WINB)
        rotary(1, 0, 2, rti, tm, 512, "tm.k")
        P.add("dve", lambda e: e.tensor_copy(V(vaugp, 0, 128, 0, [[65, 2], [1, 64]]),
                                             V(psZ[1], 0, 128, 128, [[64, 2], [1, 64]])), ["psZ1"], [f"vaug{p3}"])
        if k == "main" and t == NMAIN - 1:
            kv_out(1, kvp_d, "o.kvp")
        if k == "smp":
            kv_out(1, kvs_d, "o.kvs")
        fT_transposes(i)

    def gla_AT(i, mask_bf, mask_key):
        p = i % 2
        fTp = fT[p]
        lastmm = None
        for h in range(4):
            j, b0 = h // 2, (h % 2) * 64
            lastmm = P.add("pe", lambda e, h=h, j=j, b0=b0: e.matmul(psBk[0][:, h * 128:(h + 1) * 128],
                                                                      lhsT=fTp[b0:b0 + 64, (7 + j) * 128:(8 + j) * 128],
                                                                      rhs=fTp[b0:b0 + 64, (5 + j) * 128:(6 + j) * 128], start=True, stop=True),
                           [f"fT{p}.g", f"fT{p}.8"], ["psB0"], force=([lastmm] if lastmm is not None else ()))
        P.add("dve", lambda e: e.tensor_tensor(V(AT_bf, 0, 128, 0, [[128, 4], [1, 128]]), V(psBk[0], 0, 128, 0, [[128, 4], [1, 128]]),
                                               V(mask_bf, 0, 128, 0, [[0, 4], [1, 128]]), ALU.mult),
              ["psB0", mask_key], ["AT_bf"])

    def stage_B(i):
        k = kind(i)
        p, p3 = i % 2, i % 3
        if k == "pre":
            state_update(i, acc_decay=True)
            return
        if k == "smp":
            stage_B_sample(i)
            return
        t = i - NPRE
        fTp, vp = fT[p], v_bf[p]
        cur, prev = i % 3, (i - 1) % 3
        kTc, kTp_, vac, vap = kT[cur], kT[prev], vaug[cur], vaug[prev]
        for g in range(2):
            b0 = g * 64
            for blk, ksl, kt in ((0, prev, kTp_), (1, cur, kTc)):
                P.add("pe", lambda e, b0=b0, blk=blk, kt=kt: e.matmul(
                    psBk[blk][:], lhsT=kt[b0:b0 + 64, :], rhs=fTp[b0:b0 + 64, 0:512], start=True, stop=True),
                    [f"kT{ksl}", f"fT{p}.q"], [f"psB{blk}"])
                bias = t0b[:, 0:1] if (t == 0 and blk == 0) else 0.0
                bias_hi = t0b[64:128, 0:1] if (t == 0 and blk == 0) else 0.0
                base = (g * 2 + blk) * 512
                if blk == 0:
                    P.add("act", lambda e, base=base, bias=bias: e.activation(
                        V(pT, 0, 128, base, [[128, 4], [1, 64]]), V(psBk[0], 0, 128, 0, [[128, 4], [1, 64]]),
                        AF.Exp, scale=0.125, bias=bias), ["psB0", "t0b"], [f"pT{g}0"])
                    P.add("act", lambda e, base=base, bias_hi=bias_hi: e.activation(
                        V(pT, 64, 64, base + 64, [[128, 4], [1, 64]]), V(psBk[0], 64, 64, 64, [[128, 4], [1, 64]]),
                        AF.Exp, scale=0.125, bias=bias_hi), ["psB0", "t0b"], [f"pT{g}0"])
                else:
                    P.add("act", lambda e, base=base: e.activation(
                        pT[0:64, base:base + 512], psBk[1][0:64, :], AF.Exp, scale=0.125), ["psB1"], [f"pT{g}1"])
                    P.add("act", lambda e, base=base: e.activation(
                        V(pT, 64, 64, base + 64, [[128, 4], [1, 64]]), V(psBk[1], 64, 64, 64, [[128, 4], [1, 64]]),
                        AF.Exp, scale=0.125), ["psB1"], [f"pT{g}1"])
            for b in range(4):
                po, oo = (g * 2 + 0) * 512 + b * 128, (g * 2 + 1) * 512 + b * 128
                ocol = b * 65
                vcol = g * 65
                P.add("pe", lambda e, po=po, ocol=ocol, vcol=vcol: e.matmul(
                    psBk[2][:, ocol:ocol + 65], lhsT=pT[:, po:po + 128], rhs=vap[:, vcol:vcol + 65], start=True, stop=False),
                    [f"pT{g}0", f"vaug{prev}"], ["psB2"])
                P.add("pe", lambda e, oo=oo, ocol=ocol, vcol=vcol: e.matmul(
                    psBk[2][:, ocol:ocol + 65], lhsT=pT[:, oo:oo + 128], rhs=vac[:, vcol:vcol + 65], start=False, stop=True),
                    [f"pT{g}1", f"vaug{cur}"], ["psB2"])
            attn_finish(i, g)
        gla_AT(i, maskB_bf, "maskB_bf")
        for h in range(4):
            j, b0 = h // 2, (h % 2) * 64
            P.add("pe", lambda e, h=h, j=j, b0=b0: e.matmul(psBk[1][:, h * 128:(h + 1) * 128],
                                                             lhsT=fTp[b0:b0 + 64, (5 + j) * 128:(6 + j) * 128],
                                                             rhs=S_bf[b0:b0 + 64, j * 128:(j + 1) * 128], start=True, stop=False),
                  [f"fT{p}.g", "S_bf"], ["psB1"])
            P.add("pe", lambda e, h=h: e.matmul(psBk[1][:, h * 128:(h + 1) * 128], lhsT=AT_bf[:, h * 128:(h + 1) * 128],
                                                 rhs=vp[:, h * 128:(h + 1) * 128], start=False, stop=True),
                  ["AT_bf", f"v_bf{p}"], ["psB1"])
        state_update(i)
        gla_finish(i)
        if t == NMAIN - 1:
            dma(bass.AP(nsp_d.tensor, 0, [[128, 128], [2 * 64 * 128, 2], [1, 128]]), S[:], ["S0k", "S1k"], ["o.nsp"], "nsp")
            outkeys.append("o.nsp")
        out_proj(i, ym_d[t * 128:(t + 1) * 128, :], f"o.y{t}")

    PAIRS = [[2 * q, 2 * q + 1] for q in range(NCORES // 2)]

    def exchange():
        dma(cc_in.ap(), S[:], ["S0k", "S1k"], ["cc_in"], "xchg_in")
        P.add("pool", lambda e: e.collective_compute("AllGather", ALU.bypass, replica_groups=PAIRS,
                                                     ins=[cc_in.ap()], outs=[cc_out.ap()]),
              ["cc_in"], ["cc_out"], chan="cc_gla", cost=12000.0)
        dma(Srecv[:], cc_out.ap()[0:128, :], ["cc_out"], ["Srecv"], "srecv")
        for j in range(2):
            P.add("dve", lambda e, j=j: e.scalar_tensor_tensor(S[:, j * 128:(j + 1) * 128], Srecv[:, j * 128:(j + 1) * 128],
                                                               Dtot[:, j:j + 1], S[:, j * 128:(j + 1) * 128],
                                                               ALU.mult, ALU.add),
                  ["Srecv", "Dtot", f"S{j}k"], [f"S{j}k"])
        P.add("dve", lambda e: e.tensor_scalar(S[:], S[:], sflag[:, 0:1], None, ALU.mult), ["S0k", "S1k", "sflag"], ["S0k", "S1k"])
        P.add("dve", lambda e: e.tensor_copy(S_bf[:], S[:]), ["S0k", "S1k"], ["S_bf"])

    def sample_prep():
        dma(V(ck_f, 0, 128, 0, [[128, 4], [1, 128]]), bass.AP(ck_d.tensor, 0, [[128, 128], [128 * 128, 4], [1, 128]]), [], ["ck_f"], "ck")
        dma(V(cv_f, 0, 128, 0, [[128, 4], [1, 128]]), bass.AP(cv_d.tensor, 0, [[128, 128], [128 * 128, 4], [1, 128]]), [], ["cv_f"], "cv")
        for s_ in range(4):
            for j in range(2):
                dma(S0[:, (s_ * 2 + j) * 128:(s_ * 2 + j + 1) * 128],
                    bass.AP(st_d.tensor, s_ * 4 * 64 * 128 + j * 2 * 64 * 128, [[128, 128], [1, 128]]), [], [f"S0.{s_}.{j}"], f"S0_{s_}_{j}")
        P.add("pool", lambda e: e.tensor_copy(S0_bf[:], S0[:]), S0keys, ["S0_bf"])
        P.add("pool", lambda e: e.tensor_copy(ck_b[:], ck_f[:]), ["ck_f"], ["ck_b"])
        P.add("pool", lambda e: e.tensor_copy(V(vaug_c, 0, 128, 0, [[65, 8], [1, 64]]), V(cv_f, 0, 128, 0, [[64, 8], [1, 64]])),
              ["cv_f", "vaug_c"], ["vaug_c"])

    S0keys = [f"S0.{s_}.{j}" for s_ in range(4) for j in range(2)]

    def stage_B_sample(i):
        p, p3 = i % 2, i % 3
        fTp, vp, decp = fT[p], v_bf[p], dec[p3]
        cur = p3
        kTc, vac = kT[cur], vaug[cur]
        for s_ in range(4):
            P.add("pe", lambda e, s_=s_: e.transpose(psB0b[:, s_ * 128:(s_ + 1) * 128], ck_b[:, s_ * 128:(s_ + 1) * 128], ident[:]),
                  ["ck_b", "ident"], ["psB0"])
        P.add("dve", lambda e: e.tensor_copy(kcT[:], psB0b[:, 0:512]), ["psB0"], ["kcT"])
        for g in range(2):
            b0 = g * 64
            for s_ in range(4):
                P.add("pe", lambda e, b0=b0, s_=s_: e.matmul(
                    psBk[0][:, s_ * 128:(s_ + 1) * 128], lhsT=kcT[b0:b0 + 64, s_ * 128:(s_ + 1) * 128],
                    rhs=V(fTp, b0, 64, 32 * s_, [[128, 4], [1, 32]]), start=True, stop=True),
                    ["kcT", f"fT{p}.q"], ["psB0"])
            for s_ in range(4):
                P.add("act", lambda e, g=g, s_=s_: e.activation(
                    V(Pprev, 0, 128, ((g * 4) * 4 + s_) * 128 + 32 * s_, [[512, 4], [1, 32]]),
                    V(psBk[0], 0, 128, s_ * 128, [[32, 4], [1, 32]]), AF.Exp, scale=0.125),
                    ["psB0"], [f"Pprev{g}"])
            P.add("pe", lambda e, b0=b0: e.matmul(psBk[1][:], lhsT=kTc[b0:b0 + 64, :], rhs=fTp[b0:b0 + 64, 0:512],
                                                   start=True, stop=True), [f"kT{cur}", f"fT{p}.q"], ["psB1"])
            P.add("act", lambda e, g=g: e.activation(pTo[:, g * 512:(g + 1) * 512], psBk[1][:], AF.Exp, scale=0.125),
                  ["psB1"], [f"pTo{g}"])
            P.add("dve", lambda e, g=g: e.tensor_tensor(V(pT, 0, 128, (g * 2 + 1) * 512, [[128, 4], [1, 128]]),
                                                        V(pTo, 0, 128, g * 512, [[128, 4], [1, 128]]),
                                                        V(blockm_bf, 0, 128, 0, [[0, 4], [1, 128]]), ALU.mult),
                  [f"pTo{g}", "blockm_bf"], [f"pT{g}1"])
            for b in range(4):
                h = g * 4 + b
                ocol = b * 65
                for s_ in range(4):
                    P.add("pe", lambda e, g=g, h=h, s_=s_, ocol=ocol: e.matmul(
                        psBk[2][:, ocol:ocol + 65], lhsT=Pprev[:, (h * 4 + s_) * 128:(h * 4 + s_ + 1) * 128],
                        rhs=vaug_c[:, (s_ * 2 + g) * 65:(s_ * 2 + g + 1) * 65], start=(s_ == 0), stop=False),
                        [f"Pprev{g}", "vaug_c"], ["psB2"])
                oo = (g * 2 + 1) * 512 + b * 128
                P.add("pe", lambda e, g=g, oo=oo, ocol=ocol: e.matmul(
                    psBk[2][:, ocol:ocol + 65], lhsT=pT[:, oo:oo + 128], rhs=vac[:, g * 65:(g + 1) * 65],
                    start=False, stop=True), [f"pT{g}1", f"vaug{cur}"], ["psB2"])
            attn_finish(i, g)
        gla_AT(i, maskB32_bf, "maskB32_bf")
        for j in range(2):
            P.add("pool", lambda e, j=j: e.tensor_copy(V(Qs, 0, 128, j * 512, [[160, 4], [1, 32]]),
                                                       V(fTp, 0, 128, (5 + j) * 128, [[32, 4], [1, 32]])), [f"fT{p}.g", "Qs"], ["Qs"])
        for h in range(4):
            j, b0 = h // 2, (h % 2) * 64
            for s_ in range(4):
                P.add("pe", lambda e, h=h, j=j, b0=b0, s_=s_: e.matmul(
                    psBk[1][:, h * 128:(h + 1) * 128], lhsT=Qs[b0:b0 + 64, (j * 4 + s_) * 128:(j * 4 + s_ + 1) * 128],
                    rhs=S0_bf[b0:b0 + 64, (s_ * 2 + j) * 128:(s_ * 2 + j + 1) * 128], start=(s_ == 0), stop=False),
                    ["Qs", "S0_bf"], ["psB1"])
            P.add("pe", lambda e, h=h: e.matmul(psBk[1][:, h * 128:(h + 1) * 128], lhsT=AT_bf[:, h * 128:(h + 1) * 128],
                                                 rhs=vp[:, h * 128:(h + 1) * 128], start=False, stop=True),
                  ["AT_bf", f"v_bf{p}"], ["psB1"])
        gla_finish(i)
        psU = [psBk[0], psBk[2]]
        for s_ in range(4):
            for h in range(4):
                j, b0 = h // 2, (h % 2) * 64
                col = ((s_ % 2) * 2 + j) * 128
                P.add("pe", lambda e, s_=s_, h=h, b0=b0, col=col: e.matmul(
                    psU[s_ // 2][b0:b0 + 64, col:col + 128], lhsT=kpm[:, s_ * 256 + h * 64:s_ * 256 + (h + 1) * 64],
                    rhs=vp[:, h * 128:(h + 1) * 128], start=True, stop=True),
                    [f"kpm{s_}", f"v_bf{p}"], [["psB0"], ["psB2"]][s_ // 2])
        for s_ in range(4):
            for j in range(2):
                col = ((s_ % 2) * 2 + j) * 128
                P.add("dve", lambda e, s_=s_, j=j, col=col: e.scalar_tensor_tensor(
                    nst[:, (s_ * 2 + j) * 128:(s_ * 2 + j + 1) * 128], S0[:, (s_ * 2 + j) * 128:(s_ * 2 + j + 1) * 128],
                    decp[:, 4 * j + s_:4 * j + s_ + 1], psU[s_ // 2][:, col:col + 128], ALU.mult, ALU.add),
                    S0keys + [f"dec{p3}"] + [["psB0"], ["psB2"]][s_ // 2], [f"nst{s_}{j}"])
                dma(bass.AP(nss_d.tensor, s_ * 4 * 64 * 128 + j * 2 * 64 * 128, [[128, 128], [1, 128]]),
                    nst[:, (s_ * 2 + j) * 128:(s_ * 2 + j + 1) * 128], [f"nst{s_}{j}"], [f"o.nss{s_}{j}"], f"nss{s_}{j}")
                outkeys.append(f"o.nss{s_}{j}")
        out_proj(i, ys_d, "o.ys")

    load_w_lr()
    wjobs = []
    for k in range(8):
        wjobs.append((load_w_in, (k, CGK, CGG)))
    for k in range(8):
        wjobs.append((load_w_in, (k, CGQ, CGK)))
    for k in range(8):
        wjobs.append((load_w_in, (k, CGG, CLR)))
    for k in range(8):
        wjobs.append((load_w_in, (k, 0, 640)))
        wjobs.append((load_w_in, (k, 640, 1280)))
    for k in range(8):
        wjobs.append((load_w_out, (k, 0)))
        wjobs.append((load_w_out, (k, 1)))
    wjobs.append((sample_prep, ()))
    WPER = max(4, -(-len(wjobs) // max(1, min(NPRE, 11))))
    segW = []
    for j0 in range(0, len(wjobs), WPER):
        P.begin()
        for fn_, args_ in wjobs[j0:j0 + WPER]:
            fn_(*args_)
        segW.append(P.end())

    def capture(fn, i):
        P.begin()
        fn(i)
        return P.end()

    segA0 = [capture(stage_A0, i) for i in range(NT)]
    segA1 = [capture(stage_A1, i) for i in range(NT)]
    segA2 = [capture(stage_A2, i) for i in range(NT)]
    segB = [capture(stage_B, i) for i in range(NT)]
    if PIPELINE and SCHED == 2:
        L, PR, PRIO = {}, {}, {}
        nW = len(segW)
        for w in range(nW):
            L[("W", w)] = segW[w]
            PR[("W", w)] = {("W", w - 1)} if w > 0 else set()
            PRIO[("W", w)] = (w, 4)
        for i in range(NT):
            for st_, seg, off, rank in (("A0", segA0, 0, 3), ("A1", segA1, 1, 2), ("A2", segA2, 2, 1), ("B", segB, 3, 0)):
                L[(st_, i)] = seg[i]
                PRIO[(st_, i)] = (i + off, rank)
            pa0 = {("A0", i - 1), ("B", i - 4), ("A2", i - 3)}
            pa1 = {("A0", i), ("A1", i - 1), ("A2", i - 2), ("B", i - 3)}
            pa2 = {("A1", i), ("A2", i - 1), ("B", i - 2), ("W", min(i + 1, nW - 1))}
            pb = {("A2", i), ("B", i - 1)}
            for key, pre in ((("A0", i), pa0), (("A1", i), pa1), (("A2", i), pa2), (("B", i), pb)):
                PR[key] = set(p for p in pre if p in L or (p[0] != "W" and 0 <= p[1] < NT) or (p[0] == "W" and 0 <= p[1] < nW))
        for key in PR:
            PR[key] = set(p for p in PR[key] if (p[0] == "W" and 0 <= p[1] < nW) or (p[0] != "W" and 0 <= p[1] < NT))
        P.begin()
        exchange()
        L[("X", 0)] = P.end()
        PR[("X", 0)] = {("B", NPRE - 1)}
        PRIO[("X", 0)] = (NPRE - 1 + 3, -1)
        PR[("B", NPRE)].add(("X", 0))
        P.schedule_all(L, PR, PRIO)
    else:
        for s_ in range(NT + 3):
            lists = []
            if PIPELINE:
                if 0 <= s_ - 3 < NT:
                    lists.append(segB[s_ - 3])
                if 0 <= s_ - 2 < NT:
                    lists.append(segA2[s_ - 2])
                if 0 <= s_ - 1 < NT:
                    lists.append(segA1[s_ - 1])
                if s_ < NT:
                    lists.append(segA0[s_])
                if s_ < len(segW):
                    lists.append(segW[s_])
                if SCHED:
                    P.replay_scheduled(lists)
                else:
                    P.replay_merged(lists)
            else:
                if s_ < len(segW):
                    P.replay_merged([segW[s_]])
                if s_ < NT:
                    P.replay_merged([segA0[s_]])
                    P.replay_merged([segA1[s_]])
                    P.replay_merged([segA2[s_]])
                    P.replay_merged([segB[s_]])
    P.add("sp", None, outkeys, [])
    P.emit(nc, stack)
    stack.close()
    return nc


_CACHE = {}


def _rot_table(pos):
    half = 8
    inv = np.float32(500000.0) ** (-np.arange(half, dtype=np.float32) * np.float32(2.0 / 16))
    ang = pos.astype(np.float32)[:, None] * inv[None, :]
    c = np.cos(ang.astype(np.float64)).astype(np.float32)
    s = np.sin(ang.astype(np.float64)).astype(np.float32)
    return np.concatenate([c, c, -s, s], axis=1)


def kernel(x_prompt, x_sample, cache_k, cache_v, state_gla, norm_pre_w, w_in, attn_sinks, w_gk_up, b_gk,
           gla_norm_w, w_out, norm_post_w):
    f32 = np.float32
    x_prompt = np.asarray(x_prompt, f32)
    x_sample = np.asarray(x_sample, f32)
    if "nc" not in _CACHE:
        _CACHE["nc"] = build_program()
    nc = _CACHE["nc"]
    w_in0 = np.asarray(w_in, f32)[0]
    qperm = np.concatenate([np.arange(h * 64, (h + 1) * 64) for h in (0, 4, 1, 5, 2, 6, 3, 7)])
    perm = np.concatenate([qperm, np.arange(512, DIN)])
    w_in_p = np.ascontiguousarray(w_in0[:, perm])
    ident = np.eye(128, dtype=f32).astype(ml_dtypes.bfloat16)
    ii = np.arange(128)
    same32 = (ii[:, None] // 32) == (ii[None, :] // 32)
    maskB = (ii[:, None] <= ii[None, :]).astype(f32)
    maskC = (ii[:, None] > ii[None, :]).astype(f32)
    masks = np.ascontiguousarray(np.concatenate([maskB, maskC, maskB * same32, maskC * same32], axis=1).astype(f32))
    blockm = same32.astype(f32)
    rowm = (ii[:, None] // 32 == np.arange(4)[None, :]).astype(f32)
    common = dict(
        w_in=w_in_p, w_out=np.ascontiguousarray(np.asarray(w_out, f32)[0]),
        npw=np.ascontiguousarray(np.asarray(norm_pre_w, f32)[0].reshape(8, 128).T),
        w_lr=np.ascontiguousarray(w_in_p[:, CLR:CLR + 16].reshape(8, 128, 16).transpose(1, 0, 2).reshape(128, 128)),
        sinks=np.asarray(attn_sinks, f32)[0],
        w_up=np.asarray(w_gk_up, f32)[0], b_gk=np.asarray(b_gk, f32)[0],
        gnw=np.asarray(gla_norm_w, f32)[0], wpost=np.asarray(norm_post_w, f32)[0],
        ident=ident, masks=masks, blockm=blockm, rowm=rowm,
    )
    in_maps = []
    for c in range(NCORES):
        b, half = c // 2, c % 2
        xm = x_prompt[b, half * 2048:(half + 1) * 2048]
        xp = x_prompt[b, half * 1024:(half + 1) * 1024]
        rot = np.zeros((NMAIN + 2, 128, 32), f32)
        rot[0] = _rot_table(np.arange(1920, 2048))
        for t in range(NMAIN):
            rot[1 + t] = _rot_table(half * 2048 + t * 128 + np.arange(128))
        rot[NMAIN + 1] = _rot_table(4096 + (np.arange(128) % 32))
        t0bias = np.full((128, 1), 0.0 if half == 1 else -30000.0, f32)
        m = dict(common)
        m.update(
            xp=np.ascontiguousarray(xp), xm=np.ascontiguousarray(xm),
            xs=np.ascontiguousarray(x_sample[4 * c:4 * c + 4].reshape(128, D)),
            ck=np.ascontiguousarray(np.asarray(cache_k, f32)[0, 4 * c:4 * c + 4].reshape(4, 128, 128)),
            cv=np.ascontiguousarray(np.asarray(cache_v, f32)[0, 4 * c:4 * c + 4].reshape(4, 128, 128)),
            st=np.ascontiguousarray(np.asarray(state_gla, f32)[0, 4 * c:4 * c + 4]),
            t0bias=t0bias, rot=np.ascontiguousarray(rot.transpose(1, 0, 2).reshape(128, (NMAIN + 2) * 32)), sflag=np.full((128, 1), float(half), f32),
        )
        in_maps.append(m)
    res = run_bass_kernel_spmd(nc, in_maps, core_ids=list(range(NCORES)))
    R = res.results
    y_p = np.zeros((4, 4096, D), f32)
    y_s = np.zeros((32, 32, D), f32)
    nkp = np.zeros((1, 4, 128, 2, 64), f32)
    nvp = np.zeros((1, 4, 128, 2, 64), f32)
    nsp = np.zeros((1, 4, 4, 64, 128), f32)
    nks = np.zeros((1, 32, 32, 2, 64), f32)
    nvs = np.zeros((1, 32, 32, 2, 64), f32)
    nss = np.zeros((1, 32, 4, 64, 128), f32)
    for c in range(NCORES):
        b, half = c // 2, c % 2
        r = R[c]
        y_p[b, half * 2048:(half + 1) * 2048] = r["y_m"]
        y_s[4 * c:4 * c + 4] = r["y_s"].reshape(4, 32, D)
        if half == 1:
            nkp[0, b] = r["kv_p"][:, 0:128].reshape(128, 2, 64)
            nvp[0, b] = r["kv_p"][:, 128:256].reshape(128, 2, 64)
            nsp[0, b] = r["ns_p"]
        nks[0, 4 * c:4 * c + 4] = r["kv_s"][:, 0:128].reshape(4, 32, 2, 64)
        nvs[0, 4 * c:4 * c + 4] = r["kv_s"][:, 128:256].reshape(4, 32, 2, 64)
        nss[0, 4 * c:4 * c + 4] = r["ns_s"]
    return (y_p, y_s, nkp, nvp, nsp, nks, nvs, nss)
```
